# Optimizing a Trainium2 kernel written in Bass

```python
import math
import jax, jax.numpy as jnp
from jax import lax
import numpy as np

D_MODEL = 2048
BATCH = 8
SEQ = 2048
DEPTH = 2
DEC_BATCH = 16
DEC_SEQ = 16
PAST_LEN = 1024

CHUNK = 64
D_POOL = D_MODEL
POOL_WINDOWS = (2, 4, 8, 16)
N_POOL_GROUPS = len(POOL_WINDOWS)
POOL_GROUP = D_POOL // N_POOL_GROUPS
POOL_KEEP = max(POOL_WINDOWS) - 1
D_INNER = 2 * D_MODEL
SSD_HEAD_DIM = 64
SSD_HEADS = D_INNER // SSD_HEAD_DIM
SSD_GROUPS = 8
HEADS_PER_GROUP = SSD_HEADS // SSD_GROUPS
D_STATE = 128
CONV_W = 4
D_CONV = D_INNER + 2 * SSD_GROUPS * D_STATE
SSD_CHUNK = CHUNK
NORM_GROUP = D_INNER // SSD_GROUPS
D_FF = 4 * D_MODEL
N_BRANCH = 2
OFF_POOL = 0
OFF_Z = OFF_POOL + D_POOL
OFF_XBC = OFF_Z + D_INNER
OFF_DT = OFF_XBC + D_CONV
OFF_GATE = OFF_DT + SSD_HEADS
IN_COLS = OFF_GATE + N_BRANCH * D_MODEL
EPS = 1e-6

kernel_name = 'hybrid_pool_ssd_stream_step'


def rmsnorm(x, g):
    xf = x.astype(jnp.float32)
    y = xf * lax.rsqrt(jnp.mean(xf * xf, axis=-1, keepdims=True) + EPS)
    return (y * g.astype(jnp.float32)).astype(x.dtype)


def gated_rmsnorm(y, z, g):
    h = y.astype(jnp.float32) * jax.nn.silu(z.astype(jnp.float32))
    lead = h.shape[:-1]
    h = h.reshape(lead + (SSD_GROUPS, NORM_GROUP))
    h = h * lax.rsqrt(jnp.mean(h * h, axis=-1, keepdims=True) + EPS)
    return h.reshape(lead + (D_INNER,)) * g.astype(jnp.float32)


def pool_mixer(p, st, pos0, pool_w, pool_scale):
    L = p.shape[1]
    xp = jnp.concatenate([st.astype(p.dtype), p], axis=1)
    s = jnp.cumsum(xp.astype(jnp.float32), axis=1)
    s = jnp.pad(s, ((0, 0), (1, 0), (0, 0)))
    pos = pos0 + jnp.arange(L)
    outs = []
    for g, w in enumerate(POOL_WINDOWS):
        lo, hi = g * POOL_GROUP, (g + 1) * POOL_GROUP
        win = s[:, POOL_KEEP + 1:POOL_KEEP + 1 + L, lo:hi] - s[:, POOL_KEEP + 1 - w:POOL_KEEP + 1 - w + L, lo:hi]
        cnt = jnp.minimum(w, pos + 1).astype(jnp.float32)
        d = win / cnt[None, :, None] - p[:, :, lo:hi].astype(jnp.float32)
        outs.append(jnp.einsum('blc,cd->bld', d.astype(p.dtype), pool_w[g]))
    out = jnp.concatenate(outs, axis=-1) * pool_scale
    return out, xp[:, -POOL_KEEP:]


def causal_conv(xbc, st, w, b):
    xc = jnp.concatenate([st.astype(xbc.dtype), xbc], axis=1)
    y = lax.conv_general_dilated(xc, w[:, None, :], window_strides=(1,), padding='VALID',
                                 dimension_numbers=('NWC', 'WIO', 'NWC'), feature_group_count=D_CONV)
    return jax.nn.silu(y + b), xc[:, -(CONV_W - 1):]


def ssd_scan(xh, dt, A, Bm, Cm, h0):
    b, L = xh.shape[:2]
    pad = (-L) % SSD_CHUNK
    nc = (L + pad) // SSD_CHUNK

    def to_chunks(a):
        a = jnp.pad(a, [(0, 0), (0, pad)] + [(0, 0)] * (a.ndim - 2))
        a = a.reshape((b, nc, SSD_CHUNK) + a.shape[2:])
        return jnp.moveaxis(a, 1, 0)

    causal = jnp.tril(jnp.ones((SSD_CHUNK, SSD_CHUNK), dtype=bool))[None, :, :, None, None]

    def step(h, inp):
        xc, dtc, Bc, Cc = inp
        cum = jnp.cumsum(dtc * A, axis=1)
        seg = cum[:, :, None] - cum[:, None, :]
        decay = jnp.exp(jnp.where(causal, seg, -jnp.inf))
        cb = jnp.einsum('btgn,bsgn->btsg', Cc, Bc)
        xdt = xc * dtc[..., None]
        y = jnp.einsum('btsgr,bsgrp->btgrp', cb[..., None] * decay, xdt)
        y = y + jnp.einsum('btgn,bgrpn->btgrp', Cc, h) * jnp.exp(cum)[..., None]
        last = cum[:, -1]
        w_end = jnp.exp(last[:, None] - cum)
        h = h * jnp.exp(last)[..., None, None] + jnp.einsum('bsgn,bsgr,bsgrp->bgrpn', Bc, w_end, xdt)
        return h, y

    h, ys = lax.scan(step, h0, (to_chunks(xh), to_chunks(dt), to_chunks(Bm), to_chunks(Cm)))
    ys = jnp.moveaxis(ys, 0, 1).reshape((b, nc * SSD_CHUNK) + xh.shape[2:])
    return ys[:, :L], h


def mixer(u, st_pool, st_conv, st_ssm, pos0, p):
    b, L, _ = u.shape
    proj = u @ p['w_in']
    pool_in = proj[..., OFF_POOL:OFF_Z]
    z = proj[..., OFF_Z:OFF_XBC]
    xbc = proj[..., OFF_XBC:OFF_DT]
    dt_raw = proj[..., OFF_DT:OFF_GATE]
    gates = jax.nn.sigmoid(proj[..., OFF_GATE:].astype(jnp.float32)).reshape(b, L, N_BRANCH, D_MODEL)

    a_pool, new_pool = pool_mixer(pool_in, st_pool, pos0, p['pool_w'], p['pool_scale'])

    xbc, new_conv = causal_conv(xbc, st_conv, p['conv_w'], p['conv_b'])
    xbc = xbc.astype(jnp.float32)
    xh = xbc[..., :D_INNER].reshape(b, L, SSD_GROUPS, HEADS_PER_GROUP, SSD_HEAD_DIM)
    Bm = xbc[..., D_INNER:D_INNER + SSD_GROUPS * D_STATE].reshape(b, L, SSD_GROUPS, D_STATE)
    Cm = xbc[..., D_INNER + SSD_GROUPS * D_STATE:].reshape(b, L, SSD_GROUPS, D_STATE)
    dt = jax.nn.softplus(dt_raw.astype(jnp.float32) + p['dt_bias'].astype(jnp.float32))
    dt = dt.reshape(b, L, SSD_GROUPS, HEADS_PER_GROUP)
    A = -jnp.exp(p['a_log'].astype(jnp.float32)).reshape(SSD_GROUPS, HEADS_PER_GROUP)
    h0 = st_ssm.astype(jnp.float32).reshape(b, SSD_GROUPS, HEADS_PER_GROUP, SSD_HEAD_DIM, D_STATE)
    y, h = ssd_scan(xh, dt, A, Bm, Cm, h0)
    y = y + p['d_skip'].astype(jnp.float32).reshape(SSD_GROUPS, HEADS_PER_GROUP)[..., None] * xh
    a_ssd = gated_rmsnorm(y.reshape(b, L, D_INNER), z, p['ssd_norm']).astype(u.dtype)

    br_pool = (a_pool @ p['w_pool_proj']).astype(jnp.float32)
    br_ssd = (a_ssd @ p['w_ssd_proj']).astype(jnp.float32)
    merged = (gates[:, :, 0] * br_pool + gates[:, :, 1] * br_ssd).astype(u.dtype)
    out = merged @ p['w_out']
    new_ssm = h.reshape(b, SSD_HEADS, SSD_HEAD_DIM, D_STATE).astype(st_ssm.dtype)
    return out, new_pool, new_conv, new_ssm


def layer(x, st_pool, st_conv, st_ssm, pos0, p):
    m, new_pool, new_conv, new_ssm = mixer(rmsnorm(x, p['g_mix_pre']), st_pool, st_conv, st_ssm, pos0, p)
    x = x + rmsnorm(m, p['g_mix_post'])
    hdn = jnp.square(jax.nn.relu(rmsnorm(x, p['g_mlp_pre']) @ p['w_up'])) @ p['w_down']
    x = x + rmsnorm(hdn, p['g_mlp_post'])
    return x, new_pool, new_conv, new_ssm


def setup_inputs(seed: int = 0) -> dict:
    key = jax.random.key(seed)
    ks = jax.random.split(key, 24)
    f32 = jnp.float32
    nrm = lambda k, shape, s: jax.random.normal(k, shape, f32) * s
    dt0 = jnp.exp(jax.random.uniform(ks[9], (DEPTH, SSD_HEADS), f32, math.log(1e-3), math.log(1e-1)))
    return {
        'x_prompt': nrm(ks[0], (BATCH, SEQ, D_MODEL), 1.0),
        'x_sample': nrm(ks[1], (DEC_BATCH, DEC_SEQ, D_MODEL), 1.0),
        'state_pool': nrm(ks[2], (DEPTH, DEC_BATCH, POOL_KEEP, D_POOL), 1.0),
        'state_conv': nrm(ks[3], (DEPTH, DEC_BATCH, CONV_W - 1, D_CONV), 1.0),
        'state_ssm': nrm(ks[4], (DEPTH, DEC_BATCH, SSD_HEADS, SSD_HEAD_DIM, D_STATE), 0.1),
        'w_in': nrm(ks[5], (DEPTH, D_MODEL, IN_COLS), D_MODEL ** -0.5),
        'pool_w': nrm(ks[6], (DEPTH, N_POOL_GROUPS, POOL_GROUP, POOL_GROUP), POOL_GROUP ** -0.5),
        'pool_scale': 1.0 + nrm(ks[7], (DEPTH, D_POOL), 0.1),
        'conv_w': nrm(ks[8], (DEPTH, CONV_W, D_CONV), CONV_W ** -0.5),
        'conv_b': nrm(ks[10], (DEPTH, D_CONV), 0.01),
        'dt_bias': dt0 + jnp.log(-jnp.expm1(-dt0)),
        'a_log': jnp.log(jax.random.uniform(ks[11], (DEPTH, SSD_HEADS), f32, 1.0, 16.0)),
        'd_skip': 1.0 + nrm(ks[12], (DEPTH, SSD_HEADS), 0.1),
        'ssd_norm': 1.0 + nrm(ks[13], (DEPTH, D_INNER), 0.1),
        'w_pool_proj': nrm(ks[14], (DEPTH, D_POOL, D_MODEL), D_POOL ** -0.5),
        'w_ssd_proj': nrm(ks[15], (DEPTH, D_INNER, D_MODEL), D_INNER ** -0.5),
        'w_out': nrm(ks[16], (DEPTH, D_MODEL, D_MODEL), D_MODEL ** -0.5),
        'g_mix_pre': 1.0 + nrm(ks[17], (DEPTH, D_MODEL), 0.1),
        'g_mix_post': 1.0 + nrm(ks[18], (DEPTH, D_MODEL), 0.1),
        'w_up': nrm(ks[19], (DEPTH, D_MODEL, D_FF), D_MODEL ** -0.5),
        'w_down': nrm(ks[20], (DEPTH, D_FF, D_MODEL), D_FF ** -0.5),
        'g_mlp_pre': 1.0 + nrm(ks[21], (DEPTH, D_MODEL), 0.1),
        'g_mlp_post': 1.0 + nrm(ks[22], (DEPTH, D_MODEL), 0.1),
    }


def reference(x_prompt, x_sample, state_pool, state_conv, state_ssm, w_in, pool_w, pool_scale, conv_w, conv_b,
              dt_bias, a_log, d_skip, ssd_norm, w_pool_proj, w_ssd_proj, w_out, g_mix_pre, g_mix_post,
              w_up, w_down, g_mlp_pre, g_mlp_post):
    bp = x_prompt.shape[0]
    zero_pool = jnp.zeros((bp, POOL_KEEP, D_POOL), x_prompt.dtype)
    zero_conv = jnp.zeros((bp, CONV_W - 1, D_CONV), x_prompt.dtype)
    zero_ssm = jnp.zeros((bp, SSD_HEADS, SSD_HEAD_DIM, D_STATE), state_ssm.dtype)
    yp, ys = x_prompt, x_sample
    pool_p, conv_p, ssm_p, pool_s, conv_s, ssm_s = [], [], [], [], [], []
    for l in range(DEPTH):
        p = {'w_in': w_in[l], 'pool_w': pool_w[l], 'pool_scale': pool_scale[l], 'conv_w': conv_w[l],
             'conv_b': conv_b[l], 'dt_bias': dt_bias[l], 'a_log': a_log[l], 'd_skip': d_skip[l],
             'ssd_norm': ssd_norm[l], 'w_pool_proj': w_pool_proj[l], 'w_ssd_proj': w_ssd_proj[l],
             'w_out': w_out[l], 'g_mix_pre': g_mix_pre[l], 'g_mix_post': g_mix_post[l], 'w_up': w_up[l],
             'w_down': w_down[l], 'g_mlp_pre': g_mlp_pre[l], 'g_mlp_post': g_mlp_post[l]}
        yp, sp, sc, ss = layer(yp, zero_pool, zero_conv, zero_ssm, 0, p)
        pool_p.append(sp); conv_p.append(sc); ssm_p.append(ss)
        ys, sp, sc, ss = layer(ys, state_pool[l], state_conv[l], state_ssm[l], PAST_LEN, p)
        pool_s.append(sp); conv_s.append(sc); ssm_s.append(ss)
    return (yp, ys, jnp.stack(pool_p), jnp.stack(conv_p), jnp.stack(ssm_p),
            jnp.stack(pool_s), jnp.stack(conv_s), jnp.stack(ssm_s))
```

```python
import numpy as np
from contextlib import ExitStack
import concourse.bass as bass
import concourse.mybir as mybir
from concourse.bass_utils import run_bass_kernel_spmd

F32 = mybir.dt.float32
BF16 = mybir.dt.bfloat16
U8 = mybir.dt.uint8
AF = mybir.ActivationFunctionType
ALU = mybir.AluOpType
ENGS = ("pe", "act", "dve", "pool", "sp")

D = 2048; DI = 4096; NH = 64; HD = 64; NG = 8; NS = 128; DFF = 8192
OFF_Z = 2048; OFF_X = 6144; OFF_B = 10240; OFF_C = 11264; OFF_DT = 12288; OFF_GA = 12352; OFF_GB = 14400
IN_COLS = 16448; DCONV = 6144
EPS = 1e-6
NBLK = 4
TMAX = 544


class Res:
    __slots__ = ("name", "w", "r", "dsem", "dcnt")

    def __init__(self, name):
        self.name = name; self.w = None; self.r = []; self.dsem = None; self.dcnt = 0


class Emitter:
    def __init__(self, nc):
        self.nc = nc
        self.eng = {"pe": nc.tensor, "act": nc.scalar, "dve": nc.vector, "pool": nc.gpsimd, "sp": nc.sync}
        self.esem = {e: nc.alloc_semaphore("s_" + e) for e in ENGS}
        self.dsem = []
        self.vcount = {e: 0 for e in ENGS}
        self.known = {e: {} for e in ENGS}
        self.dlast = {}

    def _need(self, eng, ev):
        if ev is None:
            return
        if eng == "pe" and ev[0] == "e" and ev[1] == "pe":
            return
        key = (ev[0], ev[1]); val = ev[2]
        kn = self.known[eng]
        if kn.get(key, -1) >= val:
            return
        kn[key] = val
        if ev[0] == "e":
            self.eng[eng].wait_ge(self.esem[ev[1]], ev[2])
        else:
            self.eng[eng].wait_ge(self.dsem[ev[1]], ev[2])

    def _deps(self, eng, reads, writes):
        best = {}
        evs = [r.w for r in reads]
        for w in writes:
            evs.append(w.w)
            evs.extend(w.r)
        for ev in evs:
            if ev is None:
                continue
            k = (ev[0], ev[1])
            if k not in best or ev[2] > best[k][2]:
                best[k] = ev
        for ev in best.values():
            self._need(eng, ev)
        for w in writes:
            if len(w.r) > 1:
                last = {}
                for ev in w.r:
                    k = (ev[0], ev[1])
                    if k not in last or ev[2] > last[k][2]:
                        last[k] = ev
                w.r = list(last.values())

    def op(self, eng, fn, reads=(), writes=()):
        self._deps(eng, reads, writes)
        self.vcount[eng] += 1
        ev = ("e", eng, self.vcount[eng])
        fn(self.eng[eng]).then_inc(self.esem[eng], 1)
        for r in reads:
            r.r.append(ev)
        for w in writes:
            w.w = ev; w.r = []
        return ev

    def dma(self, eng, fn, sem_res, reads=(), writes=()):
        if sem_res.dsem is None:
            sem_res.dsem = len(self.dsem)
            self.dsem.append(self.nc.alloc_semaphore("d%d" % len(self.dsem)))
        self._deps(eng, reads, writes)
        sem_res.dcnt += 16
        ev = ("d", sem_res.dsem, sem_res.dcnt)
        self.dlast[sem_res.dsem] = sem_res.dcnt
        fn(self.eng[eng]).then_inc(self.dsem[sem_res.dsem], 16)
        for r in reads:
            r.r.append(ev)
        for w in writes:
            w.w = ev; w.r = []
        return ev

    def wait_all(self, eng, events):
        for ev in events:
            self._need(eng, ev)

    def barrier(self):
        evs = []
        for e in ENGS:
            if self.vcount[e] > 0:
                evs.append(("e", e, self.vcount[e]))
        for s_, c in self.dlast.items():
            evs.append(("d", s_, c))
        for e in ENGS:
            self.wait_all(e, evs)


def _isz(dt):
    return 4 if dt == F32 else (2 if dt == BF16 else 1)


class Arena:
    def __init__(self, nc, nbytes):
        self.t = nc.alloc_sbuf_tensor("arena", [128, nbytes], U8)
        self.nbytes = nbytes
        self.off = 0

    def alloc(self, shape, dtype, off=None):
        n = 1
        for s in shape[1:]:
            n *= s
        nb = n * _isz(dtype)
        if off is None:
            off = self.off
            self.off = (off + nb + 63) // 64 * 64
            assert self.off <= self.nbytes, ("arena overflow", self.off)
        v = self.t[0:shape[0], off:off + nb].bitcast(dtype)
        if len(shape) == 3:
            v = v.rearrange("p (a b) -> p a b", a=shape[1])
        elif len(shape) == 4:
            v = v.rearrange("p (a b c) -> p a b c", a=shape[1], b=shape[2])
        return v


def build_program():
    nc = bass.Bass("TRN2", target_bir_lowering=False)

    def din(name, shape):
        return nc.dram_tensor(name, shape, F32, kind="ExternalInput").ap()

    def dout(name, shape):
        return nc.dram_tensor(name, shape, F32, kind="ExternalOutput").ap()

    xp = din("xp", [2048, D]); xsm = din("xsm", [32, D])
    stp = din("stp", [2, 2, 15, D]); stc = din("stc", [2, 2, 3, DCONV]); sts = din("sts", [2, 2, DI, NS])
    w_in = din("w_in", [2, D, IN_COLS]); pool_w = din("pool_w", [2, 4, 512, 512]); pool_scale = din("pool_scale", [2, D])
    conv_w = din("conv_w", [2, 4, DCONV]); conv_b = din("conv_b", [2, DCONV]); dt_bias = din("dt_bias", [2, NH])
    a_log = din("a_log", [2, NH]); d_skip = din("d_skip", [2, NH]); ssd_norm = din("ssd_norm", [2, DI])
    w_pp = din("w_pool_proj", [2, D, D]); w_sp = din("w_ssd_proj", [2, DI, D]); w_out = din("w_out", [2, D, D])
    gvec = [din(n, [2, D]) for n in ("g_mix_pre", "g_mix_post", "g_mlp_pre", "g_mlp_post")]
    w_up = din("w_up", [2, D, DFF]); w_down = din("w_down", [2, DFF, D])
    yp = dout("yp", [2048, D]); ysm = dout("ysm", [32, D])
    npp = dout("npp", [2, 15, D]); ncp = dout("ncp", [2, 3, DCONV]); nsp = dout("nsp", [2, DI, NS])
    nps = dout("nps", [2, 2, 15, D]); ncs = dout("ncs", [2, 2, 3, DCONV]); nss = dout("nss", [2, 2, DI, NS])
    xs = nc.dram_tensor("xs_scr", [16, 128, TMAX], F32, kind="Internal").ap()
    hs = nc.dram_tensor("hs_scr", [2, NG, 128, 512], F32, kind="Internal").ap()
    NWC = 350
    wcs = [nc.dram_tensor("wc_scr%d" % l, [NWC, 128, 2048], BF16, kind="Internal").ap() for l in range(2)]
    wc_n = [0, 0]
    wc_ids = {}
    Rwc = []
    Rxs = Res("xs"); Rhs = [[Res("hs%d_%d" % (l, g)) for g in range(NG)] for l in range(2)]

    em = Emitter(nc)
    out_events = []
    st = ExitStack()
    with st:
        st.enter_context(nc.allow_non_contiguous_dma(reason="small constant / state layouts"))
        A = Arena(nc, 207 * 1024)
        PS = [nc.alloc_psum_tensor("ps%d" % i, [128, 512], F32) for i in range(8)]
        RB = [Res("psb%d" % i) for i in range(8)]

        uT = A.alloc([128, 16, TMAX], BF16); RuT = [Res("uT%d" % k) for k in range(16)]
        hbuf = A.alloc([128, 32, TMAX], BF16); Rh = [Res("hb%d" % k) for k in range(32)]
        NWS, NWB = 3, 4
        wst = [A.alloc([128, 16, 128], F32) for _ in range(NWS)]; Rwst = [Res("wst%d" % i) for i in range(NWS)]
        wbf = [A.alloc([128, 16, 128], BF16) for _ in range(NWB)]; Rwbf = [Res("wbf%d" % i) for i in range(NWB)]
        ring = {"ws": 0, "wb": 0, "cast": 0, "pb": 0, "npairs": 2}
        identf = A.alloc([128, 128], F32); Rid = Res("identf")
        ones_bf = A.alloc([128, 128], BF16); Rones = Res("ones")
        ones_f = A.alloc([64, 128], F32)
        ugt = A.alloc([64, 64], F32); mask01 = A.alloc([64, 64], F32); ugt_bf = A.alloc([64, 64], BF16)
        invc = A.alloc([128, 16], F32)
        gv = A.alloc([128, 2, 4, 16], F32)
        psc = A.alloc([128, 2, 16], F32)
        cw = A.alloc([128, 2, 4, 48], F32)
        cbias = A.alloc([128, 2, 48], F32)
        nw = A.alloc([128, 2, 32], F32)
        Dp = A.alloc([128, 2, 32], F32)
        dtb = A.alloc([128, 2, 64], F32)
        Abc = A.alloc([128, 2, 64], F32)
        Rc = Res("consts")
        pcar = [A.alloc([128, 16, 15], F32) for _ in range(2)]; Rpcar = [Res("pcar%d" % l) for l in range(2)]
        ccar = [A.alloc([128, 48, 3], F32) for _ in range(2)]; Rccar = [Res("ccar%d" % l) for l in range(2)]
        sp_pool = [A.alloc([128, 16, 15], F32) for _ in range(2)]; Rsp_pool = [Res("spp%d" % b) for b in range(2)]
        sp_conv = [A.alloc([128, 48, 3], F32) for _ in range(2)]; Rsp_conv = [Res("spc%d" % b) for b in range(2)]
        spo = [A.alloc([128, 16, 15], F32) for _ in range(2)]; Rspo = [Res("spo%d" % b) for b in range(2)]
        sco = [A.alloc([128, 48, 3], F32) for _ in range(2)]; Rsco = [Res("sco%d" % b) for b in range(2)]
        stg = [A.alloc([128, TMAX], F32) for _ in range(2)]; Rstg = [Res("stg%d" % i) for i in range(2)]
        Rxin = []
        sq = [A.alloc([128, TMAX], BF16) for _ in range(2)]; Rsq = [Res("sq%d" % i) for i in range(2)]
        rstd = A.alloc([128, TMAX], F32); Rrstd = Res("rstd")
        tst = [A.alloc([128, 512], F32) for _ in range(2)]; Rtst = [Res("tst%d" % i) for i in range(2)]
        cnt = {"stg": 0, "xin": 0, "sq": 0, "tst": 0}

        def nxt(name, n=2):
            i = cnt[name]; cnt[name] = (i + 1) % n
            return i

        big1 = A.alloc([128, 16, TMAX], F32); Rb1 = [Res("b1_%d" % k) for k in range(16)]
        big1_off = A.off - ((16 * TMAX * 4 + 63) // 64 * 64)
        ov_base = big1_off
        persist_end = A.off
        A.off = ov_base
        xg = A.alloc([128, 4, TMAX], F32); Rxgc = [Res("xgc%d" % i) for i in range(10)]
        BT = A.alloc([128, TMAX], F32); CT = A.alloc([128, TMAX], F32); RBT = Res("BT"); RCT = Res("CT")
        BTb = A.alloc([128, TMAX], BF16); CTb = A.alloc([128, TMAX], BF16); RBTb = Res("BTb"); RCTb = Res("CTb")
        PCMAX = TMAX + 9
        raw = [A.alloc([128, PCMAX], F32) for _ in range(2)]; Rraw = [Res("raw%d" % i) for i in range(2)]
        cacc = [A.alloc([128, PCMAX], F32) for _ in range(2)]; Rcacc = [Res("cacc%d" % i) for i in range(2)]
        zs4 = A.alloc([128, 4, TMAX], F32); zs = [zs4[:, i, :] for i in range(4)]; Rzs = [Res("zs%d" % i) for i in range(4)]
        Dg = A.alloc([128, 4, 128], F32); RDg = Res("Dg")
        dt_tm = A.alloc([64, 10, 64], F32); a_tm = A.alloc([64, 10, 64], F32); Rdt = Res("dt_tm")
        a_hl = A.alloc([64, 10, 2, 64], BF16)
        dtmp = A.alloc([64, 512], F32); Rdtmp = Res("dtmp")
        wdt = A.alloc([128, 16, 64], BF16); Rwdt = Res("wdt")
        hT = A.alloc([128, 512], F32); RhT = Res("hT")
        hTb2 = [A.alloc([128, 512], BF16) for _ in range(2)]; RhTb2 = [Res("hTb%d" % i) for i in range(2)]
        hstg = A.alloc([128, 4, 128], F32); Rhstg = Res("hstg")
        NCH = 3
        c_xdt = [A.alloc([64, 512], BF16) for _ in range(NCH)]; c_xw = [A.alloc([64, 512], BF16) for _ in range(NCH)]
        c_btm = [A.alloc([64, 128], BF16) for _ in range(NCH)]; c_cbm = [A.alloc([64, 64], F32) for _ in range(NCH)]
        c_R = [A.alloc([64, 1024], BF16) for _ in range(NCH)]; c_Mexp = [A.alloc([64, 512], F32) for _ in range(NCH)]
        c_Mb = [A.alloc([64, 512], BF16) for _ in range(NCH)]; c_E = [A.alloc([128, 512], F32) for _ in range(NCH)]
        c_Cs = [A.alloc([128, 512], BF16) for _ in range(NCH)]
        c_htmp = A.alloc([128, 512], F32); Rhtmp = Res("htmp")
        Rc_ = [{n: Res(n + str(i)) for n in ("xdt", "xw", "btm", "cbm", "R", "Mexp", "Mb", "E", "Cs")} for i in range(NCH)]
        ssd_end = A.off
        A.off = persist_end
        PPMAX = TMAX + 45
        praw = [A.alloc([128, PPMAX], F32) for _ in range(2)]; Rpraw = [Res("praw%d" % i) for i in range(2)]
        tA = A.alloc([128, PPMAX], F32); tB = A.alloc([128, PPMAX], F32); RtA = Res("tA"); RtB = Res("tB")
        dTt = [A.alloc([128, 4, TMAX], BF16) for _ in range(2)]; RdT = [Res("dT%d" % i) for i in range(2)]
        pool_end = A.off
        A.off = persist_end
        xpre = A.alloc([128, 16, TMAX], F32); Rxpre = Res("xpre")
        pool_end = max(pool_end, A.off)
        assert max(ssd_end, pool_end) <= A.nbytes, (ssd_end, pool_end)
        print('arena use', persist_end, ssd_end, pool_end, A.nbytes)
        cnt.update({"raw": 0, "zs": 0, "chunk": 0, "praw": 0, "dT": 0})

        def dve(fn, r, w):
            return em.op("dve", fn, reads=r, writes=w)

        def act(fn, r, w):
            return em.op("act", fn, reads=r, writes=w)

        def pool(fn, r, w):
            return em.op("pool", fn, reads=r, writes=w)

        def pe(fn, r, w):
            return em.op("pe", fn, reads=r, writes=w)

        def load(fn, sem_res, r=(), w=()):
            return em.dma("sp", fn, sem_res, reads=r, writes=w)

        pool(lambda e: e.memset(identf[:], 1.0), [], [Rid])
        pool(lambda e: e.affine_select(out=identf[:], in_=identf[:], pattern=[[-1, 128]], base=0, channel_multiplier=1,
                                       compare_op=ALU.is_equal, fill=0.0), [Rid], [Rid])
        pool(lambda e: e.memset(ones_f[:], 1.0), [], [Rc])
        pool(lambda e: e.memset(ugt[:], 1.0), [], [Rc])
        pool(lambda e: e.affine_select(out=ugt[:], in_=ugt[:], pattern=[[-1, 64]], base=0, channel_multiplier=1,
                                       compare_op=ALU.is_gt, fill=0.0), [Rc], [Rc])
        pool(lambda e: e.tensor_copy(out=ugt_bf[:], in_=ugt[:]), [Rc], [Rc])
        pool(lambda e: e.memset(mask01[:], 1.0), [], [Rc])
        pool(lambda e: e.affine_select(out=mask01[:], in_=mask01[:], pattern=[[1, 64]], base=0, channel_multiplier=-1,
                                       compare_op=ALU.is_ge, fill=0.0), [Rc], [Rc])
        pool(lambda e: e.iota(invc[:], pattern=[[1, 16]], base=1, channel_multiplier=0, allow_small_or_imprecise_dtypes=True), [Rc], [Rc])
        dve(lambda e: e.reciprocal(out=invc[:], in_=invc[:]), [Rc], [Rc])
        dve(lambda e: e.memset(ones_bf[:], 1.0), [], [Rones])
        for l in range(2):
            for i in range(4):
                load(lambda e, l=l, i=i: e.dma_start(out=gv[:, l, i, :], in_=gvec[i][l].rearrange("(k p) -> p k", p=128)), Rc, w=[Rc])
            load(lambda e, l=l: e.dma_start(out=psc[:, l, :], in_=pool_scale[l].rearrange("(k p) -> p k", p=128)), Rc, w=[Rc])
            for j in range(4):
                for q in range(3):
                    load(lambda e, l=l, j=j, q=q: e.dma_start(out=cw[:, l, j, 16 * q:16 * q + 16],
                                                               in_=conv_w[l, j, 2048 * q:2048 * q + 2048].rearrange("(k p) -> p k", p=128)), Rc, w=[Rc])
            for q in range(3):
                load(lambda e, l=l, q=q: e.dma_start(out=cbias[:, l, 16 * q:16 * q + 16],
                                                      in_=conv_b[l, 2048 * q:2048 * q + 2048].rearrange("(k p) -> p k", p=128)), Rc, w=[Rc])
            for q in range(2):
                load(lambda e, l=l, q=q: e.dma_start(out=nw[:, l, 16 * q:16 * q + 16],
                                                      in_=ssd_norm[l, 2048 * q:2048 * q + 2048].rearrange("(k p) -> p k", p=128)), Rc, w=[Rc])
            dv = d_skip[l:l + 1, :].rearrange("o (k t) -> o t k", t=2)
            for hf in range(2):
                load(lambda e, l=l, hf=hf, dv=dv: e.dma_start(out=Dp[64 * hf:64 * hf + 64, l, :], in_=dv[:, hf, :].partition_broadcast(64)[:, 0, :]), Rc, w=[Rc])
            load(lambda e, l=l: e.dma_start(out=dtb[:, l, :], in_=dt_bias[l:l + 1, :].partition_broadcast(128)[:, 0, :]), Rc, w=[Rc])
            load(lambda e, l=l: e.dma_start(out=Abc[:, l, :], in_=a_log[l:l + 1, :].partition_broadcast(128)[:, 0, :]), Rc, w=[Rc])
        act(lambda e: e.activation(out=Abc[:], in_=Abc[:], func=AF.Exp), [Rc], [Rc])
        dve(lambda e: e.tensor_scalar(out=Abc[:], in0=Abc[:], scalar1=-1.0, scalar2=None, op0=ALU.mult), [Rc], [Rc])

        def cast_eng():
            i = ring["cast"]; ring["cast"] = (i + 1) % 2
            return ("act", "dve")[i]

        def load_slab(src, nk, M, key=None):
            j = ring["wb"]; ring["wb"] = (j + 1) % NWB
            if key is not None and key in wc_ids:
                cid, wl, wi = wc_ids[key]
                load(lambda e: e.dma_start(out=wbf[j][:, 0:nk, 0:M], in_=wcs[wl][wi, :, 0:nk * M].rearrange("p (k m) -> p k m", k=nk)),
                     Rwbf[j], r=[Rwc[cid]], w=[Rwbf[j]])
                return wbf[j], Rwbf[j]
            i = ring["ws"]; ring["ws"] = (i + 1) % NWS
            load(lambda e: e.dma_start(out=wst[i][:, 0:nk, 0:M], in_=src), Rwst[i], w=[Rwst[i]])
            ce = "act" if key is not None else cast_eng()
            if ce == "act":
                act(lambda e: e.copy(out=wbf[j][:, 0:nk, 0:M], in_=wst[i][:, 0:nk, 0:M]), [Rwst[i]], [Rwbf[j]])
            else:
                em.op(ce, lambda e: e.tensor_copy(out=wbf[j][:, 0:nk, 0:M], in_=wst[i][:, 0:nk, 0:M]), reads=[Rwst[i]], writes=[Rwbf[j]])
            if key is not None and not (b == 0 and ((key[4] // 128) if len(key) > 4 else key[3]) % 4 == 3):
                wl = key[1]; wi = wc_n[wl]; wc_n[wl] += 1
                cid = len(Rwc); wc_ids[key] = (cid, wl, wi); Rwc.append(Res("wc%d" % cid))
                assert wi < NWC
                em.dma("act", lambda e: e.dma_start(out=wcs[wl][wi, :, 0:nk * M].rearrange("p (k m) -> p k m", k=nk), in_=wbf[j][:, 0:nk, 0:M]),
                       Rwbf[j], reads=[Rwbf[j]], writes=[Rwc[cid]])
            return wbf[j], Rwbf[j]

        def wslab(W, l, k0, nk, n0, M):
            return (W[l, k0:k0 + nk * 128, n0:n0 + M].rearrange("(k p) n -> p k n", p=128), (id(W.tensor) if hasattr(W, "tensor") else id(W), l, k0, nk, n0, M))

        def mm_group(slabs, M, splits):
            pb = ring["pb"]; ring["pb"] = (pb + 1) % ring["npairs"]
            total = sum(s[1] for s in slabs); idx = 0
            for (src, nk, rhs_fn) in slabs:
                if isinstance(src, tuple):
                    src, key = src
                else:
                    key = None
                wb, rwb = load_slab(src, nk, M, key)
                for k in range(nk):
                    rap, rres = rhs_fn(k)
                    for si, (c0, n) in enumerate(splits):
                        bk = 2 * pb + si
                        pe(lambda e, bk=bk, k=k, wb=wb, rap=rap, c0=c0, n=n, idx=idx: e.matmul(
                            PS[bk][0:M, 0:n], lhsT=wb[:, k, 0:M], rhs=rap[:, c0:c0 + n], start=(idx == 0), stop=(idx == total - 1)),
                           [rwb, rres], [RB[bk]])
                    idx += 1
            return pb

        def for_splits(pb, splits, fn):
            for si, (c0, n) in enumerate(splits):
                fn(PS[2 * pb + si], RB[2 * pb + si], c0, n)

        def rhs_uT(k):
            return uT[:, k, :], RuT[k]

        def rhs_h(base):
            return lambda k: (hbuf[:, base + k, :], Rh[base + k])

        def t_out(src_fn, src_res, nk, r, dst):
            for k0 in range(0, nk, 4):
                for kk in range(4):
                    pe(lambda e, k=k0 + kk, kk=kk: e.transpose(out=PS[4][0:r, kk * 128:(kk + 1) * 128], in_=src_fn(k), identity=identf[:]),
                       [src_res, Rid], [RB[4]])
                i = nxt("tst")
                act(lambda e, i=i: e.copy(out=tst[i][0:r, :], in_=PS[4][0:r, :]), [RB[4]], [Rtst[i]])
                out_events.append(load(lambda e, i=i, k0=k0: e.dma_start(out=dst[:, k0 * 128:k0 * 128 + 512], in_=tst[i][0:r, :]), Rtst[i], r=[Rtst[i]]))

        def t_in(src, r, nk, dst_fn, dst_res):
            for k0 in range(0, nk, 4):
                i = nxt("tst")
                load(lambda e, i=i, k0=k0: e.dma_start(out=tst[i][0:r, :], in_=src[:, k0 * 128:k0 * 128 + 512]), Rtst[i], w=[Rtst[i]])
                for kk in range(4):
                    pe(lambda e, i=i, kk=kk: e.transpose(out=PS[4][:, kk * 16:kk * 16 + r], in_=tst[i][0:r, kk * 128:(kk + 1) * 128], identity=identf[0:r, 0:r]),
                       [Rtst[i], Rid], [RB[4]])
                for kk in range(4):
                    act(lambda e, k=k0 + kk, kk=kk: e.copy(out=dst_fn(k), in_=PS[4][:, kk * 16:kk * 16 + r]), [RB[4]], [dst_res])

        def rms_rstd(src_fn, nchunks, T, splits, nfeat):
            pb = ring["pb"]; ring["pb"] = (pb + 1) % ring["npairs"]
            for k in range(nchunks):
                ap, rs = src_fn(k)
                i = nxt("sq")
                if k % 2 == 0:
                    act(lambda e, ap=ap, i=i: e.activation(out=sq[i][:, 0:T], in_=ap, func=AF.Square), rs if isinstance(rs, list) else [rs], [Rsq[i]])
                else:
                    dve(lambda e, ap=ap, i=i: e.tensor_tensor(out=sq[i][:, 0:T], in0=ap, in1=ap, op=ALU.mult), rs if isinstance(rs, list) else [rs], [Rsq[i]])
                for si, (c0, n) in enumerate(splits):
                    pe(lambda e, i=i, si=si, c0=c0, n=n, k=k: e.matmul(PS[2 * pb + si][:, 0:n], lhsT=ones_bf[:], rhs=sq[i][:, c0:c0 + n],
                                                                      start=(k == 0), stop=(k == nchunks - 1)), [Rsq[i], Rones], [RB[2 * pb + si]])

            def ev(ps, rb, c0, n):
                dve(lambda e: e.tensor_scalar(out=rstd[:, c0:c0 + n], in0=ps[:, 0:n], scalar1=1.0 / nfeat, scalar2=EPS, op0=ALU.mult, op1=ALU.add), [rb], [Rrstd])
            for_splits(pb, splits, ev)
            act(lambda e: e.activation(out=rstd[:, 0:T], in_=rstd[:, 0:T], func=AF.Ln), [Rrstd], [Rrstd])
            act(lambda e: e.activation(out=rstd[:, 0:T], in_=rstd[:, 0:T], func=AF.Exp, scale=-0.5), [Rrstd], [Rrstd])

        def norm_to_uT(l, gi, T, splits):
            rms_rstd(lambda k: (big1[:, k, 0:T], Rb1[k]), 16, T, splits, float(D))
            for k in range(16):
                dve(lambda e, k=k: e.scalar_tensor_tensor(out=uT[:, k, 0:T], in0=big1[:, k, 0:T], scalar=gv[:, l, gi, k:k + 1], in1=rstd[:, 0:T],
                                                          op0=ALU.mult, op1=ALU.mult), [Rb1[k], Rrstd, Rc], [RuT[k]])

        def store_x(T):
            for k in range(16):
                load(lambda e, k=k: e.dma_start(out=xs[k, :, 0:T], in_=big1[:, k, 0:T]), Rb1[k], r=[Rb1[k]], w=[Rxs])

        def prefetch_x(T):
            load(lambda e: e.dma_start(out=xpre[:, :, 0:T], in_=xs[:, :, 0:T].rearrange("k p t -> p k t")), Rxpre, r=[Rxs],
                 w=[Rxpre, RtA, RtB] + Rpraw + RdT)

        def post_norm_residual(l, gi, T, splits):
            rms_rstd(lambda k: (big1[:, k, 0:T], Rb1[k]), 16, T, splits, float(D))
            for k in range(16):
                j = nxt("stg")
                dve(lambda e, k=k, j=j: e.scalar_tensor_tensor(out=stg[j][:, 0:T], in0=big1[:, k, 0:T], scalar=gv[:, l, gi, k:k + 1], in1=rstd[:, 0:T],
                                                               op0=ALU.mult, op1=ALU.mult), [Rb1[k], Rrstd, Rc], [Rstg[j]])
                em.op("pool" if k % 3 == 0 else "dve", lambda e, k=k, j=j: e.tensor_tensor(out=big1[:, k, 0:T], in0=stg[j][:, 0:T], in1=xpre[:, k, 0:T], op=ALU.add),
                      reads=[Rstg[j], Rxpre], writes=[Rb1[k]])

        for b in range(NBLK):
            lastp = (b == NBLK - 1)
            has_s = (b == 0)
            T = 544 if has_s else 512
            splits = [(0, 512), (512, 32)] if has_s else [(0, 512)]
            segs = [("p", 0, 512)] + ([("a", 512, 16), ("b", 528, 16)] if has_s else [])
            nseg = len(segs)
            chunks = [(0, 64 * c, 64) for c in range(8)] + ([(1, 512, 16), (2, 528, 16)] if has_s else [])

            ttiles = [(xp, 512 * b + 128 * t, 128, 128 * t) for t in range(4)] + ([(xsm, 0, 32, 512)] if has_s else [])
            for (srcT, r0, nr, c0) in ttiles:
                for q in range(4):
                    i = nxt("tst")
                    load(lambda e, i=i, q=q, srcT=srcT, r0=r0, nr=nr: e.dma_start(out=tst[i][0:nr, :], in_=srcT[r0:r0 + nr, 512 * q:512 * q + 512]), Rtst[i], w=[Rtst[i]])
                    for kk in range(4):
                        pe(lambda e, i=i, kk=kk, nr=nr: e.transpose(out=PS[4][:, kk * 128:kk * 128 + nr], in_=tst[i][0:nr, kk * 128:(kk + 1) * 128], identity=identf[0:nr, 0:nr]),
                           [Rtst[i], Rid], [RB[4]])
                    for kk in range(4):
                        act(lambda e, k=4 * q + kk, kk=kk, nr=nr, c0=c0: e.copy(out=big1[:, k, c0:c0 + nr], in_=PS[4][:, kk * 128:kk * 128 + nr]), [RB[4]], [Rb1[4 * q + kk]])

            for l in range(2):
                store_x(T)
                norm_to_uT(l, 0, T, splits)
                em.barrier()
                ring["npairs"] = 2; ring["pb"] = 0
                if has_s:
                    for sb in range(2):
                        t_in(stc[l, sb], 3, 48, lambda k, sb=sb: sp_conv[sb][:, k, :], Rsp_conv[sb])
                wsrc, wkey = wslab(w_in, l, 0, 16, OFF_DT, 64)
                wb_, rwb_ = load_slab(wsrc, 16, 64, wkey)
                act(lambda e, wb_=wb_: e.copy(out=wdt[:], in_=wb_[:, :, 0:64]), [rwb_], [Rwdt])
                grp = [(5, 64, [(ci, c0) for ci, (sid, c0, Q) in enumerate(chunks) if Q == 64])]
                if has_s:
                    grp.append((6, 16, [(ci, c0) for ci, (sid, c0, Q) in enumerate(chunks) if Q == 16]))
                for (bk, Q, lst) in grp:
                    nq = len(lst); ci0 = lst[0][0]
                    for qi, (ci, c0) in enumerate(lst):
                        for k in range(16):
                            pe(lambda e, k=k, c0=c0, Q=Q, qi=qi, bk=bk: e.matmul(PS[bk][0:Q, qi * 64:(qi + 1) * 64], lhsT=uT[:, k, c0:c0 + Q], rhs=wdt[:, k, :],
                                                                                 start=(k == 0), stop=(k == 15)), [RuT[k], Rwdt], [RB[bk]])
                    dve(lambda e, Q=Q, nq=nq, bk=bk: e.tensor_tensor(out=dtmp[0:Q, 0:nq * 64].rearrange("q (c h) -> q c h", c=nq),
                                                                     in0=PS[bk][0:Q, 0:nq * 64].rearrange("q (c h) -> q c h", c=nq),
                                                                     in1=dtb[0:Q, l, :][:, None, :].broadcast_to([Q, nq, 64]), op=ALU.add), [RB[bk], Rc], [Rdtmp])
                    act(lambda e, Q=Q, nq=nq: e.activation(out=dtmp[0:Q, 0:nq * 64], in_=dtmp[0:Q, 0:nq * 64], func=AF.Exp), [Rdtmp], [Rdtmp])
                    act(lambda e, Q=Q, nq=nq, ci0=ci0: e.activation(out=dt_tm[0:Q, ci0:ci0 + nq, :], in_=dtmp[0:Q, 0:nq * 64].rearrange("q (c h) -> q c h", c=nq),
                                                                    func=AF.Ln, bias=1.0), [Rdtmp], [Rdt])
                    dve(lambda e, Q=Q, nq=nq, ci0=ci0: e.tensor_tensor(out=a_tm[0:Q, ci0:ci0 + nq, :], in0=dt_tm[0:Q, ci0:ci0 + nq, :],
                                                                       in1=Abc[0:Q, l, :][:, None, :].broadcast_to([Q, nq, 64]), op=ALU.mult), [Rdt, Rc], [Rdt])
                    dve(lambda e, Q=Q, nq=nq, ci0=ci0: e.tensor_copy(out=a_hl[0:Q, ci0:ci0 + nq, 0, :], in_=a_tm[0:Q, ci0:ci0 + nq, :]), [Rdt], [Rdt])
                    dve(lambda e, Q=Q, nq=nq, ci0=ci0: e.tensor_tensor(out=a_hl[0:Q, ci0:ci0 + nq, 1, :], in0=a_tm[0:Q, ci0:ci0 + nq, :],
                                                                       in1=a_hl[0:Q, ci0:ci0 + nq, 0, :], op=ALU.subtract), [Rdt], [Rdt])

                def conv_chunk(l, kc, dst_fn, dst_res):
                    pb = mm_group([(wslab(w_in, l, 0, 16, OFF_X + 128 * kc, 128), 16, rhs_uT)], 128, splits)
                    i = nxt("raw")
                    for si, (kind, c0, n) in enumerate(segs):
                        p0 = c0 + 3 * si
                        if kind == "p":
                            if b == 0:
                                pool(lambda e, i=i, p0=p0: e.memset(raw[i][:, p0:p0 + 3], 0.0), [], [Rraw[i]])
                            else:
                                pool(lambda e, i=i, p0=p0: e.tensor_copy(out=raw[i][:, p0:p0 + 3], in_=ccar[l][:, kc, :]), [Rccar[l]], [Rraw[i]])
                        else:
                            sb = 0 if kind == "a" else 1
                            pool(lambda e, i=i, p0=p0, sb=sb: e.tensor_copy(out=raw[i][:, p0:p0 + 3], in_=sp_conv[sb][:, kc, :]), [Rsp_conv[sb]], [Rraw[i]])
                    for si, (kind, c0, n) in enumerate(segs):
                        p0 = c0 + 3 * si + 3
                        bk = 2 * pb + (0 if c0 < 512 else 1)
                        cc = c0 if c0 < 512 else c0 - 512
                        act(lambda e, i=i, p0=p0, n=n, bk=bk, cc=cc: e.copy(out=raw[i][:, p0:p0 + n], in_=PS[bk][:, cc:cc + n]), [RB[bk]], [Rraw[i]])
                    for si, (kind, c0, n) in enumerate(segs):
                        pe_ = c0 + 3 * si + 3 + n
                        if kind == "p":
                            pool(lambda e, i=i, pe_=pe_: e.tensor_copy(out=ccar[l][:, kc, :], in_=raw[i][:, pe_ - 3:pe_]), [Rraw[i]], [Rccar[l]])
                        else:
                            sb = 0 if kind == "a" else 1
                            pool(lambda e, i=i, pe_=pe_, sb=sb: e.tensor_copy(out=sco[sb][:, kc, :], in_=raw[i][:, pe_ - 3:pe_]), [Rraw[i]], [Rsco[sb]])
                    return lambda: conv_tail(l, kc, i, dst_fn, dst_res)

                def conv_tail(l, kc, i, dst_fn, dst_res):
                    L = T + 3 * nseg - 3
                    veng = "dve"
                    em.op(veng, lambda e, i=i: e.tensor_scalar(out=cacc[i][:, 0:L], in0=raw[i][:, 0:L], scalar1=cw[:, l, 0, kc:kc + 1], scalar2=None, op0=ALU.mult),
                          reads=[Rraw[i], Rc], writes=[Rcacc[i]])
                    for j in range(1, 4):
                        em.op(veng, lambda e, i=i, j=j: e.scalar_tensor_tensor(out=cacc[i][:, 0:L], in0=raw[i][:, j:j + L], scalar=cw[:, l, j, kc:kc + 1],
                                                                              in1=cacc[i][:, 0:L], op0=ALU.mult, op1=ALU.add),
                              reads=[Rraw[i], Rcacc[i], Rc], writes=[Rcacc[i]])
                    for si, (kind, c0, n) in enumerate(segs):
                        a0 = c0 + 3 * si
                        act(lambda e, i=i, a0=a0, n=n, c0=c0: e.activation(out=dst_fn(c0, n), in_=cacc[i][:, a0:a0 + n], func=AF.Silu, bias=cbias[:, l, kc:kc + 1]),
                            [Rcacc[i], Rc], dst_res if isinstance(dst_res, list) else [dst_res])

                for g in range(NG):
                    jobs = [(4 * g + j, (lambda c0, n, j=j: xg[:, j, c0:c0 + n]), Rxgc[0:len(chunks)]) for j in range(4)]
                    jobs += [(32 + g, (lambda c0, n: BT[:, c0:c0 + n]), RBT), (40 + g, (lambda c0, n: CT[:, c0:c0 + n]), RCT)]
                    pend = None
                    for (kc_, dfn, dres) in jobs:
                        tail = conv_chunk(l, kc_, dfn, dres)
                        if pend is not None:
                            pend()
                        pend = tail
                    pend()
                    act(lambda e: e.copy(out=BTb[:, 0:T], in_=BT[:, 0:T]), [RBT], [RBTb])
                    act(lambda e: e.copy(out=CTb[:, 0:T], in_=CT[:, 0:T]), [RCT], [RCTb])

                    for j in range(4):
                        pool(lambda e, j=j: e.tensor_scalar(out=Dg[:, j, :], in0=identf[:], scalar1=Dp[:, l, 4 * g + j:4 * g + j + 1], scalar2=None, op0=ALU.mult),
                             [Rid, Rc], [RDg])
                    NC_ = len(chunks)
                    st_ = {"cur": -1, "fin": None}
                    BU = 4 if has_s else 1
                    BC = 7 if has_s else 3

                    def prep(ci, g=g):
                        sid, c0, Q = chunks[ci]
                        cc = ci % NCH
                        RC = Rc_[cc]
                        xdt, xw, btm, cbm, Rm, Mexp, Mb, Et, Cs = c_xdt[cc], c_xw[cc], c_btm[cc], c_cbm[cc], c_R[cc], c_Mexp[cc], c_Mb[cc], c_E[cc], c_Cs[cc]
                        Q8 = 8 * Q
                        for j in range(4):
                            pe(lambda e, j=j: e.transpose(out=PS[4][0:Q, j * 128:(j + 1) * 128], in_=xg[:, j, c0:c0 + Q], identity=identf[:]), [Rxgc[ci], Rid], [RB[4]])
                        dve(lambda e: e.tensor_tensor(out=xdt[0:Q, :].rearrange("q (h p) -> q h p", h=8), in0=PS[4][0:Q, :].rearrange("q (h p) -> q h p", h=8),
                                                      in1=dt_tm[0:Q, ci, 8 * g:8 * g + 8][:, :, None].broadcast_to([Q, 8, 64]), op=ALU.mult),
                            [RB[4], Rdt], [RC["xdt"]])
                        pe(lambda e: e.transpose(out=PS[BC][0:Q, 320:448], in_=BT[:, c0:c0 + Q], identity=identf[:]), [RBT, Rid], [RB[BC]])
                        act(lambda e: e.copy(out=btm[0:Q, :], in_=PS[BC][0:Q, 320:448]), [RB[BC]], [RC["btm"]])
                        pe(lambda e: e.matmul(PS[BC][0:Q, 256:256 + Q], lhsT=BTb[:, c0:c0 + Q], rhs=CTb[:, c0:c0 + Q], start=True, stop=True), [RBTb, RCTb], [RB[BC]])
                        dve(lambda e: e.tensor_tensor(out=cbm[0:Q, 0:Q], in0=PS[BC][0:Q, 256:256 + Q], in1=mask01[0:Q, 0:Q], op=ALU.mult), [RB[BC], Rc], [RC["cbm"]])
                        pool(lambda e: e.tensor_tensor(out=Rm[0:Q, 0:2 * Q8].rearrange("q (w h t) -> q w h t", w=2, h=8),
                                                       in0=a_hl[0:Q, ci, :, 8 * g:8 * g + 8][:, :, :, None].broadcast_to([Q, 2, 8, Q]),
                                                       in1=mask01[0:Q, 0:Q][:, None, None, :].broadcast_to([Q, 2, 8, Q]), op=ALU.mult),
                             [Rdt, Rc], [RC["R"]])
                        for w_ in range(2):
                            pe(lambda e, w_=w_: e.matmul(PS[5][0:Q, 0:Q8], lhsT=ugt_bf[0:Q, 0:Q], rhs=Rm[0:Q, w_ * Q8:(w_ + 1) * Q8], start=(w_ == 0), stop=(w_ == 1)),
                               [RC["R"], Rc], [RB[5]])
                        for w_ in range(2):
                            pe(lambda e, w_=w_: e.matmul(PS[6][:, 0:Q8], lhsT=ones_bf[0:Q, :], rhs=Rm[0:Q, w_ * Q8:(w_ + 1) * Q8], start=(w_ == 0), stop=(w_ == 1)),
                               [RC["R"], Rones], [RB[6]])
                        act(lambda e: e.activation(out=Mexp[0:Q, 0:Q8], in_=PS[5][0:Q, 0:Q8], func=AF.Exp), [RB[5]], [RC["Mexp"]])
                        act(lambda e: e.activation(out=Et[:, 0:Q8], in_=PS[6][:, 0:Q8], func=AF.Exp), [RB[6]], [RC["E"]])

                    def prep_b(ci, g=g):
                        sid, c0, Q = chunks[ci]
                        cc = ci % NCH
                        RC = Rc_[cc]
                        xdt, xw, btm, cbm, Rm, Mexp, Mb, Et, Cs = c_xdt[cc], c_xw[cc], c_btm[cc], c_cbm[cc], c_R[cc], c_Mexp[cc], c_Mb[cc], c_E[cc], c_Cs[cc]
                        Q8 = 8 * Q
                        dve(lambda e: e.tensor_tensor(out=Mb[0:Q, 0:Q8].rearrange("q (h t) -> q h t", h=8), in0=Mexp[0:Q, 0:Q8].rearrange("q (h t) -> q h t", h=8),
                                                      in1=cbm[0:Q, 0:Q][:, None, :].broadcast_to([Q, 8, Q]), op=ALU.mult),
                            [RC["Mexp"], RC["cbm"]], [RC["Mb"]])
                        pool(lambda e: e.tensor_tensor(out=Cs[:, 0:Q8].rearrange("n (h t) -> n h t", h=8), in0=Et[:, 0:Q8].rearrange("n (h t) -> n h t", h=8),
                                                       in1=CT[:, c0:c0 + Q][:, None, :].broadcast_to([128, 8, Q]), op=ALU.mult),
                             [RC["E"], RCT], [RC["Cs"]])
                        pool(lambda e: e.tensor_tensor(out=xw[0:Q, :].rearrange("q (h p) -> q h p", h=8), in0=xdt[0:Q, :].rearrange("q (h p) -> q h p", h=8),
                                                       in1=Mexp[0:Q, 0:Q8].rearrange("q (h t) -> q h t", h=8)[:, :, Q - 1:Q].broadcast_to([Q, 8, 64]), op=ALU.mult),
                             [RC["xdt"], RC["Mexp"]], [RC["xw"]])

                    def finish_state(sid_, g=g):
                        if sid_ == 0 and not lastp:
                            load(lambda e: e.dma_start(out=hs[l, g], in_=hT[:]), RhT, r=[RhT], w=[Rhs[l][g]])
                            return
                        dstT = nsp[l] if sid_ == 0 else nss[l, sid_ - 1]
                        for jj in range(4):
                            pe(lambda e, jj=jj: e.transpose(out=PS[4][:, jj * 128:(jj + 1) * 128], in_=hT[:, jj * 128:(jj + 1) * 128], identity=identf[:]), [RhT, Rid], [RB[4]])
                        act(lambda e: e.copy(out=hstg[:].rearrange("p j n -> p (j n)"), in_=PS[4][:, :]), [RB[4]], [Rhstg])
                        out_events.append(load(lambda e: e.dma_start(out=dstT[512 * g:512 * g + 512, :].rearrange("(j p) n -> p j n", p=128), in_=hstg[:]),
                                               Rhstg, r=[Rhstg]))

                    def scan(ci, g=g):
                        sid, c0, Q = chunks[ci]
                        if sid != st_["cur"]:
                            if st_["cur"] >= 0:
                                finish_state(st_["cur"])
                            st_["cur"] = sid
                            if sid == 0:
                                if b == 0:
                                    dve(lambda e: e.memset(hT[:], 0.0), [], [RhT])
                                else:
                                    load(lambda e: e.dma_start(out=hT[:], in_=hs[l, g]), RhT, r=[Rhs[l][g]], w=[RhT])
                            else:
                                sb = sid - 1
                                load(lambda e: e.dma_start(out=hstg[:], in_=sts[l, sb, 512 * g:512 * g + 512, :].rearrange("(j p) n -> p j n", p=128)),
                                     Rhstg, w=[Rhstg])
                                for jj in range(4):
                                    pe(lambda e, jj=jj: e.transpose(out=PS[4][:, jj * 128:(jj + 1) * 128], in_=hstg[:, jj, :], identity=identf[:]), [Rhstg, Rid], [RB[4]])
                                dve(lambda e: e.tensor_copy(out=hT[:], in_=PS[4][:, :]), [RB[4]], [RhT])
                            dve(lambda e: e.tensor_copy(out=hTb2[ci % 2][:], in_=hT[:]), [RhT], [RhTb2[ci % 2]])
                        cc = ci % NCH
                        RC = Rc_[cc]
                        xdt, xw, btm, Mb, Et, Cs = c_xdt[cc], c_xw[cc], c_btm[cc], c_Mb[cc], c_E[cc], c_Cs[cc]
                        Q8 = 8 * Q
                        hb_old, rhb_old = hTb2[ci % 2], RhTb2[ci % 2]
                        hb_new, rhb_new = hTb2[(ci + 1) % 2], RhTb2[(ci + 1) % 2]
                        pe(lambda e: e.matmul(PS[BU][:, :], lhsT=btm[0:Q, :], rhs=xw[0:Q, :], start=True, stop=True), [RC["btm"], RC["xw"]], [RB[BU]])
                        for j in range(4):
                            pe(lambda e, j=j: e.matmul(PS[7][:, j * Q:(j + 1) * Q], lhsT=Dg[:, j, :], rhs=xg[:, j, c0:c0 + Q], start=(j == 0), stop=False, skip_group_check=True),
                               [RDg, Rxgc[ci]], [RB[7]])
                        for hh in range(8):
                            po = (hh % 2) * 64; jq = hh // 2
                            pe(lambda e, hh=hh, po=po, jq=jq: e.matmul(PS[7][po:po + 64, jq * Q:(jq + 1) * Q], lhsT=xdt[0:Q, hh * 64:(hh + 1) * 64],
                                                                      rhs=Mb[0:Q, hh * Q:(hh + 1) * Q], start=False, stop=False, skip_group_check=True),
                               [RC["xdt"], RC["Mb"]], [RB[7]])
                        for hh in range(8):
                            po = (hh % 2) * 64; jq = hh // 2
                            pe(lambda e, hh=hh, po=po, jq=jq: e.matmul(PS[7][po:po + 64, jq * Q:(jq + 1) * Q], lhsT=hb_old[:, hh * 64:(hh + 1) * 64],
                                                                      rhs=Cs[:, hh * Q:(hh + 1) * Q], start=False, stop=True, skip_group_check=True),
                               [rhb_old, RC["Cs"]], [RB[7]])
                        dve(lambda e: e.tensor_tensor(out=c_htmp[:].rearrange("n (h p) -> n h p", h=8), in0=hT[:].rearrange("n (h p) -> n h p", h=8),
                                                      in1=Et[:, 0:Q8].rearrange("n (h t) -> n h t", h=8)[:, :, Q - 1:Q].broadcast_to([128, 8, 64]), op=ALU.mult),
                            [RhT, RC["E"]], [Rhtmp])
                        dve(lambda e: e.tensor_tensor(out=hT[:], in0=c_htmp[:], in1=PS[BU][:, :], op=ALU.add), [Rhtmp, RB[BU]], [RhT])
                        dve(lambda e: e.tensor_copy(out=hb_new[:], in_=hT[:]), [RhT], [rhb_new])
                        dve(lambda e: e.tensor_tensor(out=zs4[:, :, c0:c0 + Q], in0=PS[7][:, 0:4 * Q].rearrange("p (j t) -> p j t", j=4), in1=zs4[:, :, c0:c0 + Q], op=ALU.mult),
                            [RB[7]] + Rzs, Rzs)

                    def z_chunk(j, g=g):
                        pb = mm_group([(wslab(w_in, l, 0, 16, OFF_Z + 512 * g + 128 * j, 128), 16, rhs_uT)], 128, splits)

                        def ev(ps, rb, c0, n):
                            act(lambda e: e.activation(out=zs[j][:, c0:c0 + n], in_=ps[:, 0:n], func=AF.Silu), [rb], [Rzs[j]])
                        for_splits(pb, splits, ev)

                    for j in range(4):
                        z_chunk(j)
                    prep(0); prep(1); prep_b(0)
                    for ci in range(NC_):
                        if ci + 2 < NC_:
                            prep(ci + 2)
                        scan(ci)
                        if ci + 1 < NC_:
                            prep_b(ci + 1)
                    finish_state(st_["cur"])

                    rms_rstd(lambda k: (zs[k][:, 0:T], Rzs[k]), 4, T, splits, 512.0)
                    for j in range(4):
                        dve(lambda e, j=j: e.scalar_tensor_tensor(out=hbuf[:, 4 * g + j, 0:T], in0=zs[j][:, 0:T], scalar=nw[:, l, 4 * g + j:4 * g + j + 1], in1=rstd[:, 0:T],
                                                                  op0=ALU.mult, op1=ALU.mult), [Rzs[j], Rrstd, Rc], [Rh[4 * g + j]])

                if lastp:
                    t_out(lambda k: ccar[l][:, k, :], Rccar[l], 48, 3, ncp[l])
                if has_s:
                    for sb in range(2):
                        t_out(lambda k, sb=sb: sco[sb][:, k, :], Rsco[sb], 48, 3, ncs[l, sb])
                em.barrier()
                ring["npairs"] = 4

                for n in range(16):
                    pbx = mm_group([(wslab(w_sp, l, 2048 * q, 16, 128 * n, 128), 16, rhs_h(16 * q)) for q in range(2)], 128, splits)
                    pby = mm_group([(wslab(w_in, l, 0, 16, OFF_GB + 128 * n, 128), 16, rhs_uT)], 128, splits)
                    i = nxt("stg")
                    for si, (c0, nn) in enumerate(splits):
                        act(lambda e, i=i, si=si, c0=c0, nn=nn: e.activation(out=stg[i][:, c0:c0 + nn], in_=PS[2 * pby + si][:, 0:nn], func=AF.Sigmoid), [RB[2 * pby + si]], [Rstg[i]])
                        dve(lambda e, i=i, si=si, c0=c0, nn=nn, n=n: e.tensor_tensor(out=big1[:, n, c0:c0 + nn], in0=stg[i][:, c0:c0 + nn], in1=PS[2 * pbx + si][:, 0:nn], op=ALU.mult),
                            [Rstg[i], RB[2 * pbx + si]], [Rb1[n]])

                if has_s:
                    for sb in range(2):
                        t_in(stp[l, sb], 15, 16, lambda k, sb=sb: sp_pool[sb][:, k, :], Rsp_pool[sb])
                for gi in range(4):
                    wnd = 2 << gi
                    di = nxt("dT")
                    for j in range(4):
                        kc = 4 * gi + j
                        pb = mm_group([(wslab(w_in, l, 0, 16, 128 * kc, 128), 16, rhs_uT)], 128, splits)
                        i = nxt("praw")
                        for si, (kind, c0, n) in enumerate(segs):
                            p0 = c0 + 15 * si
                            if kind == "p":
                                if b == 0:
                                    pool(lambda e, i=i, p0=p0: e.memset(praw[i][:, p0:p0 + 15], 0.0), [], [Rpraw[i]])
                                else:
                                    pool(lambda e, i=i, p0=p0, kc=kc: e.tensor_copy(out=praw[i][:, p0:p0 + 15], in_=pcar[l][:, kc, :]), [Rpcar[l]], [Rpraw[i]])
                            else:
                                sb = 0 if kind == "a" else 1
                                pool(lambda e, i=i, p0=p0, sb=sb, kc=kc: e.tensor_copy(out=praw[i][:, p0:p0 + 15], in_=sp_pool[sb][:, kc, :]), [Rsp_pool[sb]], [Rpraw[i]])
                        for si, (kind, c0, n) in enumerate(segs):
                            p0 = c0 + 15 * si + 15
                            bk = 2 * pb + (0 if c0 < 512 else 1)
                            cc_ = c0 if c0 < 512 else c0 - 512
                            act(lambda e, i=i, p0=p0, n=n, bk=bk, cc_=cc_: e.copy(out=praw[i][:, p0:p0 + n], in_=PS[bk][:, cc_:cc_ + n]), [RB[bk]], [Rpraw[i]])
                        for si, (kind, c0, n) in enumerate(segs):
                            pe_ = c0 + 15 * si + 15 + n
                            if kind == "p":
                                pool(lambda e, i=i, pe_=pe_, kc=kc: e.tensor_copy(out=pcar[l][:, kc, :], in_=praw[i][:, pe_ - 15:pe_]), [Rpraw[i]], [Rpcar[l]])
                            else:
                                sb = 0 if kind == "a" else 1
                                pool(lambda e, i=i, pe_=pe_, sb=sb, kc=kc: e.tensor_copy(out=spo[sb][:, kc, :], in_=praw[i][:, pe_ - 15:pe_]), [Rpraw[i]], [Rspo[sb]])
                        P = T + 15 * nseg
                        src, rsrc = praw[i], Rpraw[i]
                        bufs = [(tA, RtA), (tB, RtB)]
                        sh = 1; bi = 0; valid = 0
                        while sh < wnd:
                            dstb, rdst = bufs[bi]; bi = 1 - bi
                            v0 = valid + sh
                            dve(lambda e, src=src, dstb=dstb, v0=v0, sh=sh, P=P: e.tensor_tensor(out=dstb[:, v0:P], in0=src[:, v0:P], in1=src[:, v0 - sh:P - sh], op=ALU.add),
                                [rsrc], [rdst])
                            src, rsrc = dstb, rdst
                            valid = v0; sh *= 2
                        for si, (kind, c0, n) in enumerate(segs):
                            p0 = c0 + 15 * si + 15
                            dve(lambda e, src=src, p0=p0, n=n, c0=c0, i=i, j=j, di=di, wnd=wnd: e.scalar_tensor_tensor(
                                out=dTt[di][:, j, c0:c0 + n], in0=src[:, p0:p0 + n], scalar=1.0 / wnd, in1=praw[i][:, p0:p0 + n], op0=ALU.mult, op1=ALU.subtract),
                                [rsrc, Rpraw[i]], [RdT[di]])
                        if b == 0 and wnd > 2:
                            m = wnd - 1
                            js = nxt("stg")
                            dve(lambda e, src=src, m=m, js=js: e.tensor_tensor(out=stg[js][:, 0:m], in0=src[:, 15:15 + m], in1=invc[:, 0:m], op=ALU.mult), [rsrc, Rc], [Rstg[js]])
                            dve(lambda e, m=m, js=js, i=i, j=j, di=di: e.tensor_tensor(out=dTt[di][:, j, 0:m], in0=stg[js][:, 0:m], in1=praw[i][:, 15:15 + m], op=ALU.subtract),
                                [Rstg[js], Rpraw[i]], [RdT[di]])
                        elif b == 0:
                            js = nxt("stg")
                            dve(lambda e, src=src, js=js: e.tensor_tensor(out=stg[js][:, 0:1], in0=src[:, 15:16], in1=invc[:, 0:1], op=ALU.mult), [rsrc, Rc], [Rstg[js]])
                            dve(lambda e, js=js, i=i, j=j, di=di: e.tensor_tensor(out=dTt[di][:, j, 0:1], in0=stg[js][:, 0:1], in1=praw[i][:, 15:16], op=ALU.subtract),
                                [Rstg[js], Rpraw[i]], [RdT[di]])
                    for jo in range(4):
                        srcw = (pool_w[l, gi, :, 128 * jo:128 * jo + 128].rearrange("(k p) n -> p k n", p=128), ("pool_w", l, gi, jo))
                        pb = mm_group([(srcw, 4, lambda k, di=di: (dTt[di][:, k, :], RdT[di]))], 128, splits)

                        def ev(ps, rb, c0, n, jo=jo):
                            act(lambda e: e.activation(out=hbuf[:, 4 * gi + jo, c0:c0 + n], in_=ps[:, 0:n], func=AF.Copy, scale=psc[:, l, 4 * gi + jo:4 * gi + jo + 1]),
                                [rb, Rc], [Rh[4 * gi + jo]])
                        for_splits(pb, splits, ev)
                if lastp:
                    t_out(lambda k: pcar[l][:, k, :], Rpcar[l], 16, 15, npp[l])
                if has_s:
                    for sb in range(2):
                        t_out(lambda k, sb=sb: spo[sb][:, k, :], Rspo[sb], 16, 15, nps[l, sb])

                for n in range(16):
                    pbx = mm_group([(wslab(w_pp, l, 0, 16, 128 * n, 128), 16, rhs_h(0))], 128, splits)
                    pby = mm_group([(wslab(w_in, l, 0, 16, OFF_GA + 128 * n, 128), 16, rhs_uT)], 128, splits)
                    i = nxt("stg")
                    for si, (c0, nn) in enumerate(splits):
                        act(lambda e, i=i, si=si, c0=c0, nn=nn: e.activation(out=stg[i][:, c0:c0 + nn], in_=PS[2 * pby + si][:, 0:nn], func=AF.Sigmoid), [RB[2 * pby + si]], [Rstg[i]])
                        dve(lambda e, i=i, si=si, c0=c0, nn=nn: e.tensor_tensor(out=stg[i][:, c0:c0 + nn], in0=stg[i][:, c0:c0 + nn], in1=PS[2 * pbx + si][:, 0:nn], op=ALU.mult),
                            [Rstg[i], RB[2 * pbx + si]], [Rstg[i]])
                    pool(lambda e, i=i, n=n: e.tensor_tensor(out=hbuf[:, 16 + n, 0:T], in0=stg[i][:, 0:T], in1=big1[:, n, 0:T], op=ALU.add), [Rstg[i], Rb1[n]], [Rh[16 + n]])

                prefetch_x(T)
                for n in range(16):
                    pb = mm_group([(wslab(w_out, l, 0, 16, 128 * n, 128), 16, rhs_h(16))], 128, splits)

                    def ev(ps, rb, c0, nn, n=n):
                        act(lambda e: e.copy(out=big1[:, n, c0:c0 + nn], in_=ps[:, 0:nn]), [rb], [Rb1[n]])
                    for_splits(pb, splits, ev)
                post_norm_residual(l, 1, T, splits)
                store_x(T)

                norm_to_uT(l, 2, T, splits)
                for hh in range(2):
                    for n in range(32):
                        pb = mm_group([(wslab(w_up, l, 0, 16, 4096 * hh + 128 * n, 128), 16, rhs_uT)], 128, splits)
                        i = nxt("stg")

                        def ev(ps, rb, c0, nn, i=i):
                            act(lambda e: e.activation(out=stg[i][:, c0:c0 + nn], in_=ps[:, 0:nn], func=AF.Relu), [rb], [Rstg[i]])
                        for_splits(pb, splits, ev)
                        pool(lambda e, i=i, n=n: e.tensor_tensor(out=hbuf[:, n, 0:T], in0=stg[i][:, 0:T], in1=stg[i][:, 0:T], op=ALU.mult), [Rstg[i]], [Rh[n]])
                    if hh == 1:
                        prefetch_x(T)
                    for n in range(16):
                        pb = mm_group([(wslab(w_down, l, 4096 * hh + 2048 * q, 16, 128 * n, 128), 16, rhs_h(16 * q)) for q in range(2)], 128, splits)

                        def ev(ps, rb, c0, nn, n=n, hh=hh):
                            if hh == 0:
                                act(lambda e: e.copy(out=big1[:, n, c0:c0 + nn], in_=ps[:, 0:nn]), [rb], [Rb1[n]])
                            else:
                                dve(lambda e: e.tensor_tensor(out=big1[:, n, c0:c0 + nn], in0=big1[:, n, c0:c0 + nn], in1=ps[:, 0:nn], op=ALU.add), [rb, Rb1[n]], [Rb1[n]])
                        for_splits(pb, splits, ev)
                post_norm_residual(l, 3, T, splits)

            for (srcT, r0, nr, c0) in ttiles:
                dstT = yp if srcT is xp else ysm
                for q in range(4):
                    for kk in range(4):
                        pe(lambda e, k=4 * q + kk, kk=kk, nr=nr, c0=c0: e.transpose(out=PS[4][0:nr, kk * 128:(kk + 1) * 128], in_=big1[:, k, c0:c0 + nr], identity=identf[:]),
                           [Rb1[4 * q + kk], Rid], [RB[4]])
                    i = nxt("tst")
                    act(lambda e, i=i, nr=nr: e.copy(out=tst[i][0:nr, :], in_=PS[4][0:nr, :]), [RB[4]], [Rtst[i]])
                    out_events.append(load(lambda e, i=i, q=q, dstT=dstT, r0=r0, nr=nr: e.dma_start(out=dstT[r0:r0 + nr, 512 * q:512 * q + 512], in_=tst[i][0:nr, :]),
                                           Rtst[i], r=[Rtst[i]]))

        em.wait_all("sp", out_events)

    return nc


_NC_CACHE = {}


def kernel(x_prompt, x_sample, state_pool, state_conv, state_ssm, w_in, pool_w, pool_scale, conv_w, conv_b,
           dt_bias, a_log, d_skip, ssd_norm, w_pool_proj, w_ssd_proj, w_out, g_mix_pre, g_mix_post,
           w_up, w_down, g_mlp_pre, g_mlp_post):
    f = lambda a: np.ascontiguousarray(np.asarray(a, dtype=np.float32))
    x_prompt = f(x_prompt); x_sample = f(x_sample); state_pool = f(state_pool); state_conv = f(state_conv); state_ssm = f(state_ssm)
    shared = {"w_in": f(w_in), "pool_w": f(pool_w), "pool_scale": f(pool_scale), "conv_w": f(conv_w), "conv_b": f(conv_b),
              "dt_bias": f(dt_bias), "a_log": f(a_log), "d_skip": f(d_skip), "ssd_norm": f(ssd_norm),
              "w_pool_proj": f(w_pool_proj), "w_ssd_proj": f(w_ssd_proj), "w_out": f(w_out),
              "g_mix_pre": f(g_mix_pre), "g_mix_post": f(g_mix_post), "g_mlp_pre": f(g_mlp_pre), "g_mlp_post": f(g_mlp_post),
              "w_up": f(w_up), "w_down": f(w_down)}
    if "nc" not in _NC_CACHE:
        _NC_CACHE["nc"] = build_program()
    nc = _NC_CACHE["nc"]
    in_maps = []
    for i in range(8):
        m = dict(shared)
        m["xp"] = x_prompt[i]
        m["xsm"] = np.ascontiguousarray(x_sample[2 * i:2 * i + 2].reshape(32, D))
        m["stp"] = np.ascontiguousarray(state_pool[:, 2 * i:2 * i + 2])
        m["stc"] = np.ascontiguousarray(state_conv[:, 2 * i:2 * i + 2])
        m["sts"] = np.ascontiguousarray(state_ssm[:, 2 * i:2 * i + 2].reshape(2, 2, DI, NS))
        in_maps.append(m)
    res = run_bass_kernel_spmd(nc, in_maps, core_ids=list(range(8)))
    R = res.results
    y_p = np.stack([R[i]["yp"] for i in range(8)], 0)
    y_s = np.concatenate([R[i]["ysm"].reshape(2, 16, D) for i in range(8)], 0)
    npp = np.stack([R[i]["npp"] for i in range(8)], 1)
    ncp = np.stack([R[i]["ncp"] for i in range(8)], 1)
    nsp = np.stack([R[i]["nsp"].reshape(2, NH, HD, NS) for i in range(8)], 1)
    nps = np.concatenate([R[i]["nps"] for i in range(8)], 1)
    ncs = np.concatenate([R[i]["ncs"] for i in range(8)], 1)
    nss = np.concatenate([R[i]["nss"].reshape(2, 2, NH, HD, NS) for i in range(8)], 1)
    return (y_p.astype(np.float32), y_s.astype(np.float32), npp.astype(np.float32), ncp.astype(np.float32),
            nsp.astype(np.float32), nps.astype(np.float32), ncs.astype(np.float32), nss.astype(np.float32))
```

```python
import numpy as np
from contextlib import ExitStack
import concourse.bass as bass
import concourse.mybir as mybir
from concourse.bass_utils import run_bass_kernel_spmd

F32 = mybir.dt.float32
BF16 = mybir.dt.bfloat16
U8 = mybir.dt.uint8
AF = mybir.ActivationFunctionType
ALU = mybir.AluOpType
ENGS = ("pe", "act", "dve", "pool", "sp")

D = 2048; DI = 4096; NH = 64; HD = 64; NG = 8; NS = 128; DFF = 8192
OFF_Z = 2048; OFF_X = 6144; OFF_B = 10240; OFF_C = 11264; OFF_DT = 12288; OFF_GA = 12352; OFF_GB = 14400
IN_COLS = 16448; DCONV = 6144
EPS = 1e-6
NBLK = 4
TMAX = 544


class Res:
    __slots__ = ("name", "w", "r", "dsem", "dcnt")

    def __init__(self, name):
        self.name = name; self.w = None; self.r = []; self.dsem = None; self.dcnt = 0


class Emitter:
    def __init__(self, nc):
        self.nc = nc
        self.eng = {"pe": nc.tensor, "act": nc.scalar, "dve": nc.vector, "pool": nc.gpsimd, "sp": nc.sync}
        self.esem = {e: nc.alloc_semaphore("s_" + e) for e in ENGS}
        self.dsem = []
        self.vcount = {e: 0 for e in ENGS}
        self.known = {e: {} for e in ENGS}
        self.dlast = {}

    def _need(self, eng, ev):
        if ev is None:
            return
        if eng == "pe" and ev[0] == "e" and ev[1] == "pe":
            return
        key = (ev[0], ev[1]); val = ev[2]
        kn = self.known[eng]
        if kn.get(key, -1) >= val:
            return
        kn[key] = val
        if ev[0] == "e":
            self.eng[eng].wait_ge(self.esem[ev[1]], ev[2])
        else:
            self.eng[eng].wait_ge(self.dsem[ev[1]], ev[2])

    def _deps(self, eng, reads, writes):
        for r in reads:
            self._need(eng, r.w)
        for w in writes:
            self._need(eng, w.w)
            for ev in w.r:
                self._need(eng, ev)

    def op(self, eng, fn, reads=(), writes=()):
        self._deps(eng, reads, writes)
        self.vcount[eng] += 1
        ev = ("e", eng, self.vcount[eng])
        fn(self.eng[eng]).then_inc(self.esem[eng], 1)
        for r in reads:
            r.r.append(ev)
        for w in writes:
            w.w = ev; w.r = []
        return ev

    def dma(self, eng, fn, sem_res, reads=(), writes=()):
        if sem_res.dsem is None:
            sem_res.dsem = len(self.dsem)
            self.dsem.append(self.nc.alloc_semaphore("d%d" % len(self.dsem)))
        self._deps(eng, reads, writes)
        sem_res.dcnt += 16
        ev = ("d", sem_res.dsem, sem_res.dcnt)
        self.dlast[sem_res.dsem] = sem_res.dcnt
        fn(self.eng[eng]).then_inc(self.dsem[sem_res.dsem], 16)
        for r in reads:
            r.r.append(ev)
        for w in writes:
            w.w = ev; w.r = []
        return ev

    def wait_all(self, eng, events):
        for ev in events:
            self._need(eng, ev)

    def barrier(self):
        evs = []
        for e in ENGS:
            if self.vcount[e] > 0:
                evs.append(("e", e, self.vcount[e]))
        for s_, c in self.dlast.items():
            evs.append(("d", s_, c))
        for e in ENGS:
            self.wait_all(e, evs)


def _isz(dt):
    return 4 if dt == F32 else (2 if dt == BF16 else 1)


class Arena:
    def __init__(self, nc, nbytes):
        self.t = nc.alloc_sbuf_tensor("arena", [128, nbytes], U8)
        self.nbytes = nbytes
        self.off = 0

    def alloc(self, shape, dtype, off=None):
        n = 1
        for s in shape[1:]:
            n *= s
        nb = n * _isz(dtype)
        if off is None:
            off = self.off
            self.off = (off + nb + 63) // 64 * 64
            assert self.off <= self.nbytes, ("arena overflow", self.off)
        v = self.t[0:shape[0], off:off + nb].bitcast(dtype)
        if len(shape) == 3:
            v = v.rearrange("p (a b) -> p a b", a=shape[1])
        elif len(shape) == 4:
            v = v.rearrange("p (a b c) -> p a b c", a=shape[1], b=shape[2])
        return v


def build_program():
    nc = bass.Bass("TRN2", target_bir_lowering=False)

    def din(name, shape):
        return nc.dram_tensor(name, shape, F32, kind="ExternalInput").ap()

    def dout(name, shape):
        return nc.dram_tensor(name, shape, F32, kind="ExternalOutput").ap()

    xp = din("xp", [2048, D]); xsm = din("xsm", [32, D])
    stp = din("stp", [2, 2, 15, D]); stc = din("stc", [2, 2, 3, DCONV]); sts = din("sts", [2, 2, DI, NS])
    w_in = din("w_in", [2, D, IN_COLS]); pool_w = din("pool_w", [2, 4, 512, 512]); pool_scale = din("pool_scale", [2, D])
    conv_w = din("conv_w", [2, 4, DCONV]); conv_b = din("conv_b", [2, DCONV]); dt_bias = din("dt_bias", [2, NH])
    a_log = din("a_log", [2, NH]); d_skip = din("d_skip", [2, NH]); ssd_norm = din("ssd_norm", [2, DI])
    w_pp = din("w_pool_proj", [2, D, D]); w_sp = din("w_ssd_proj", [2, DI, D]); w_out = din("w_out", [2, D, D])
    gvec = [din(n, [2, D]) for n in ("g_mix_pre", "g_mix_post", "g_mlp_pre", "g_mlp_post")]
    w_up = din("w_up", [2, D, DFF]); w_down = din("w_down", [2, DFF, D])
    yp = dout("yp", [2048, D]); ysm = dout("ysm", [32, D])
    npp = dout("npp", [2, 15, D]); ncp = dout("ncp", [2, 3, DCONV]); nsp = dout("nsp", [2, DI, NS])
    nps = dout("nps", [2, 2, 15, D]); ncs = dout("ncs", [2, 2, 3, DCONV]); nss = dout("nss", [2, 2, DI, NS])
    xs = nc.dram_tensor("xs_scr", [16, 128, TMAX], F32, kind="Internal").ap()
    hs = nc.dram_tensor("hs_scr", [2, NG, 128, 512], F32, kind="Internal").ap()
    NWC = 350
    wcs = [nc.dram_tensor("wc_scr%d" % l, [NWC, 128, 2048], BF16, kind="Internal").ap() for l in range(2)]
    wc_n = [0, 0]
    wc_ids = {}
    Rwc = []
    Rxs = Res("xs"); Rhs = [[Res("hs%d_%d" % (l, g)) for g in range(NG)] for l in range(2)]

    em = Emitter(nc)
    out_events = []
    st = ExitStack()
    with st:
        st.enter_context(nc.allow_non_contiguous_dma(reason="small constant / state layouts"))
        A = Arena(nc, 207 * 1024)
        PS = [nc.alloc_psum_tensor("ps%d" % i, [128, 512], F32) for i in range(8)]
        RB = [Res("psb%d" % i) for i in range(8)]

        uT = A.alloc([128, 16, TMAX], BF16); RuT = [Res("uT%d" % k) for k in range(16)]
        hbuf = A.alloc([128, 32, TMAX], BF16); Rh = [Res("hb%d" % k) for k in range(32)]
        NWS, NWB = 3, 4
        wst = [A.alloc([128, 16, 128], F32) for _ in range(NWS)]; Rwst = [Res("wst%d" % i) for i in range(NWS)]
        wbf = [A.alloc([128, 16, 128], BF16) for _ in range(NWB)]; Rwbf = [Res("wbf%d" % i) for i in range(NWB)]
        ring = {"ws": 0, "wb": 0, "cast": 0, "pb": 0, "npairs": 2}
        identf = A.alloc([128, 128], F32); Rid = Res("identf")
        ones_bf = A.alloc([128, 128], BF16); Rones = Res("ones")
        ones_f = A.alloc([64, 128], F32)
        ugt = A.alloc([64, 64], F32); mask01 = A.alloc([64, 64], F32); ugt_bf = A.alloc([64, 64], BF16)
        invc = A.alloc([128, 16], F32)
        gv = A.alloc([128, 2, 4, 16], F32)
        psc = A.alloc([128, 2, 16], F32)
        cw = A.alloc([128, 2, 4, 48], F32)
        cbias = A.alloc([128, 2, 48], F32)
        nw = A.alloc([128, 2, 32], F32)
        Dp = A.alloc([128, 2, 32], F32)
        dtb = A.alloc([128, 2, 64], F32)
        Abc = A.alloc([128, 2, 64], F32)
        Rc = Res("consts")
        pcar = [A.alloc([128, 16, 15], F32) for _ in range(2)]; Rpcar = [Res("pcar%d" % l) for l in range(2)]
        ccar = [A.alloc([128, 48, 3], F32) for _ in range(2)]; Rccar = [Res("ccar%d" % l) for l in range(2)]
        sp_pool = [A.alloc([128, 16, 15], F32) for _ in range(2)]; Rsp_pool = [Res("spp%d" % b) for b in range(2)]
        sp_conv = [A.alloc([128, 48, 3], F32) for _ in range(2)]; Rsp_conv = [Res("spc%d" % b) for b in range(2)]
        spo = [A.alloc([128, 16, 15], F32) for _ in range(2)]; Rspo = [Res("spo%d" % b) for b in range(2)]
        sco = [A.alloc([128, 48, 3], F32) for _ in range(2)]; Rsco = [Res("sco%d" % b) for b in range(2)]
        stg = [A.alloc([128, TMAX], F32) for _ in range(2)]; Rstg = [Res("stg%d" % i) for i in range(2)]
        Rxin = []
        sq = [A.alloc([128, TMAX], BF16) for _ in range(2)]; Rsq = [Res("sq%d" % i) for i in range(2)]
        rstd = A.alloc([128, TMAX], F32); Rrstd = Res("rstd")
        tst = [A.alloc([128, 512], F32) for _ in range(2)]; Rtst = [Res("tst%d" % i) for i in range(2)]
        cnt = {"stg": 0, "xin": 0, "sq": 0, "tst": 0}

        def nxt(name, n=2):
            i = cnt[name]; cnt[name] = (i + 1) % n
            return i

        big1 = A.alloc([128, 16, TMAX], F32); Rb1 = [Res("b1_%d" % k) for k in range(16)]
        big1_off = A.off - ((16 * TMAX * 4 + 63) // 64 * 64)
        ov_base = big1_off
        persist_end = A.off
        A.off = ov_base
        xg = A.alloc([128, 4, TMAX], F32); Rxgc = [Res("xgc%d" % i) for i in range(10)]
        BT = A.alloc([128, TMAX], F32); CT = A.alloc([128, TMAX], F32); RBT = Res("BT"); RCT = Res("CT")
        BTb = A.alloc([128, TMAX], BF16); CTb = A.alloc([128, TMAX], BF16); RBTb = Res("BTb"); RCTb = Res("CTb")
        PCMAX = TMAX + 9
        raw = [A.alloc([128, PCMAX], F32) for _ in range(2)]; Rraw = [Res("raw%d" % i) for i in range(2)]
        cacc = [A.alloc([128, PCMAX], F32) for _ in range(2)]; Rcacc = [Res("cacc%d" % i) for i in range(2)]
        zs4 = A.alloc([128, 4, TMAX], F32); zs = [zs4[:, i, :] for i in range(4)]; Rzs = [Res("zs%d" % i) for i in range(4)]
        Dg = A.alloc([128, 4, 128], F32); RDg = Res("Dg")
        dt_tm = A.alloc([64, 10, 64], F32); a_tm = A.alloc([64, 10, 64], F32); Rdt = Res("dt_tm")
        a_hl = A.alloc([64, 10, 2, 64], BF16)
        dtmp = A.alloc([64, 512], F32); Rdtmp = Res("dtmp")
        wdt = A.alloc([128, 16, 64], BF16); Rwdt = Res("wdt")
        hT = A.alloc([128, 512], F32); RhT = Res("hT")
        hTb2 = [A.alloc([128, 512], BF16) for _ in range(2)]; RhTb2 = [Res("hTb%d" % i) for i in range(2)]
        hstg = A.alloc([128, 4, 128], F32); Rhstg = Res("hstg")
        NCH = 3
        c_xdt = [A.alloc([64, 512], BF16) for _ in range(NCH)]; c_xw = [A.alloc([64, 512], BF16) for _ in range(NCH)]
        c_btm = [A.alloc([64, 128], BF16) for _ in range(NCH)]; c_cbm = [A.alloc([64, 64], F32) for _ in range(NCH)]
        c_R = [A.alloc([64, 1024], BF16) for _ in range(NCH)]; c_Mexp = [A.alloc([64, 512], F32) for _ in range(NCH)]
        c_Mb = [A.alloc([64, 512], BF16) for _ in range(NCH)]; c_E = [A.alloc([128, 512], F32) for _ in range(NCH)]
        c_Cs = [A.alloc([128, 512], BF16) for _ in range(NCH)]
        c_htmp = A.alloc([128, 512], F32); Rhtmp = Res("htmp")
        Rc_ = [{n: Res(n + str(i)) for n in ("xdt", "xw", "btm", "cbm", "R", "Mexp", "Mb", "E", "Cs")} for i in range(NCH)]
        ssd_end = A.off
        A.off = persist_end
        PPMAX = TMAX + 45
        praw = [A.alloc([128, PPMAX], F32) for _ in range(2)]; Rpraw = [Res("praw%d" % i) for i in range(2)]
        tA = A.alloc([128, PPMAX], F32); tB = A.alloc([128, PPMAX], F32); RtA = Res("tA"); RtB = Res("tB")
        dTt = [A.alloc([128, 4, TMAX], BF16) for _ in range(2)]; RdT = [Res("dT%d" % i) for i in range(2)]
        pool_end = A.off
        A.off = persist_end
        xpre = A.alloc([128, 16, TMAX], F32); Rxpre = Res("xpre")
        pool_end = max(pool_end, A.off)
        assert max(ssd_end, pool_end) <= A.nbytes, (ssd_end, pool_end)
        print('arena use', persist_end, ssd_end, pool_end, A.nbytes)
        cnt.update({"raw": 0, "zs": 0, "chunk": 0, "praw": 0, "dT": 0})

        def dve(fn, r, w):
            return em.op("dve", fn, reads=r, writes=w)

        def act(fn, r, w):
            return em.op("act", fn, reads=r, writes=w)

        def pool(fn, r, w):
            return em.op("pool", fn, reads=r, writes=w)

        def pe(fn, r, w):
            return em.op("pe", fn, reads=r, writes=w)

        def load(fn, sem_res, r=(), w=()):
            return em.dma("sp", fn, sem_res, reads=r, writes=w)

        pool(lambda e: e.memset(identf[:], 1.0), [], [Rid])
        pool(lambda e: e.affine_select(out=identf[:], in_=identf[:], pattern=[[-1, 128]], base=0, channel_multiplier=1,
                                       compare_op=ALU.is_equal, fill=0.0), [Rid], [Rid])
        pool(lambda e: e.memset(ones_f[:], 1.0), [], [Rc])
        pool(lambda e: e.memset(ugt[:], 1.0), [], [Rc])
        pool(lambda e: e.affine_select(out=ugt[:], in_=ugt[:], pattern=[[-1, 64]], base=0, channel_multiplier=1,
                                       compare_op=ALU.is_gt, fill=0.0), [Rc], [Rc])
        pool(lambda e: e.tensor_copy(out=ugt_bf[:], in_=ugt[:]), [Rc], [Rc])
        pool(lambda e: e.memset(mask01[:], 1.0), [], [Rc])
        pool(lambda e: e.affine_select(out=mask01[:], in_=mask01[:], pattern=[[1, 64]], base=0, channel_multiplier=-1,
                                       compare_op=ALU.is_ge, fill=0.0), [Rc], [Rc])
        pool(lambda e: e.iota(invc[:], pattern=[[1, 16]], base=1, channel_multiplier=0, allow_small_or_imprecise_dtypes=True), [Rc], [Rc])
        dve(lambda e: e.reciprocal(out=invc[:], in_=invc[:]), [Rc], [Rc])
        dve(lambda e: e.memset(ones_bf[:], 1.0), [], [Rones])
        for l in range(2):
            for i in range(4):
                load(lambda e, l=l, i=i: e.dma_start(out=gv[:, l, i, :], in_=gvec[i][l].rearrange("(k p) -> p k", p=128)), Rc, w=[Rc])
            load(lambda e, l=l: e.dma_start(out=psc[:, l, :], in_=pool_scale[l].rearrange("(k p) -> p k", p=128)), Rc, w=[Rc])
            for j in range(4):
                for q in range(3):
                    load(lambda e, l=l, j=j, q=q: e.dma_start(out=cw[:, l, j, 16 * q:16 * q + 16],
                                                               in_=conv_w[l, j, 2048 * q:2048 * q + 2048].rearrange("(k p) -> p k", p=128)), Rc, w=[Rc])
            for q in range(3):
                load(lambda e, l=l, q=q: e.dma_start(out=cbias[:, l, 16 * q:16 * q + 16],
                                                      in_=conv_b[l, 2048 * q:2048 * q + 2048].rearrange("(k p) -> p k", p=128)), Rc, w=[Rc])
            for q in range(2):
                load(lambda e, l=l, q=q: e.dma_start(out=nw[:, l, 16 * q:16 * q + 16],
                                                      in_=ssd_norm[l, 2048 * q:2048 * q + 2048].rearrange("(k p) -> p k", p=128)), Rc, w=[Rc])
            dv = d_skip[l:l + 1, :].rearrange("o (k t) -> o t k", t=2)
            for hf in range(2):
                load(lambda e, l=l, hf=hf, dv=dv: e.dma_start(out=Dp[64 * hf:64 * hf + 64, l, :], in_=dv[:, hf, :].partition_broadcast(64)[:, 0, :]), Rc, w=[Rc])
            load(lambda e, l=l: e.dma_start(out=dtb[:, l, :], in_=dt_bias[l:l + 1, :].partition_broadcast(128)[:, 0, :]), Rc, w=[Rc])
            load(lambda e, l=l: e.dma_start(out=Abc[:, l, :], in_=a_log[l:l + 1, :].partition_broadcast(128)[:, 0, :]), Rc, w=[Rc])
        act(lambda e: e.activation(out=Abc[:], in_=Abc[:], func=AF.Exp), [Rc], [Rc])
        dve(lambda e: e.tensor_scalar(out=Abc[:], in0=Abc[:], scalar1=-1.0, scalar2=None, op0=ALU.mult), [Rc], [Rc])

        def cast_eng():
            i = ring["cast"]; ring["cast"] = (i + 1) % 2
            return ("act", "dve")[i]

        def load_slab(src, nk, M, key=None):
            j = ring["wb"]; ring["wb"] = (j + 1) % NWB
            if key is not None and key in wc_ids:
                cid, wl, wi = wc_ids[key]
                load(lambda e: e.dma_start(out=wbf[j][:, 0:nk, 0:M], in_=wcs[wl][wi, :, 0:nk * M].rearrange("p (k m) -> p k m", k=nk)),
                     Rwbf[j], r=[Rwc[cid]], w=[Rwbf[j]])
                return wbf[j], Rwbf[j]
            i = ring["ws"]; ring["ws"] = (i + 1) % NWS
            load(lambda e: e.dma_start(out=wst[i][:, 0:nk, 0:M], in_=src), Rwst[i], w=[Rwst[i]])
            ce = "act" if key is not None else cast_eng()
            if ce == "act":
                act(lambda e: e.copy(out=wbf[j][:, 0:nk, 0:M], in_=wst[i][:, 0:nk, 0:M]), [Rwst[i]], [Rwbf[j]])
            else:
                em.op(ce, lambda e: e.tensor_copy(out=wbf[j][:, 0:nk, 0:M], in_=wst[i][:, 0:nk, 0:M]), reads=[Rwst[i]], writes=[Rwbf[j]])
            if key is not None:
                wl = key[1]; wi = wc_n[wl]; wc_n[wl] += 1
                cid = len(Rwc); wc_ids[key] = (cid, wl, wi); Rwc.append(Res("wc%d" % cid))
                assert wi < NWC
                em.dma("act", lambda e: e.dma_start(out=wcs[wl][wi, :, 0:nk * M].rearrange("p (k m) -> p k m", k=nk), in_=wbf[j][:, 0:nk, 0:M]),
                       Rwbf[j], reads=[Rwbf[j]], writes=[Rwc[cid]])
            return wbf[j], Rwbf[j]

        def wslab(W, l, k0, nk, n0, M):
            return (W[l, k0:k0 + nk * 128, n0:n0 + M].rearrange("(k p) n -> p k n", p=128), (id(W.tensor) if hasattr(W, "tensor") else id(W), l, k0, nk, n0, M))

        def mm_group(slabs, M, splits):
            pb = ring["pb"]; ring["pb"] = (pb + 1) % ring["npairs"]
            total = sum(s[1] for s in slabs); idx = 0
            for (src, nk, rhs_fn) in slabs:
                if isinstance(src, tuple):
                    src, key = src
                else:
                    key = None
                wb, rwb = load_slab(src, nk, M, key)
                for k in range(nk):
                    rap, rres = rhs_fn(k)
                    for si, (c0, n) in enumerate(splits):
                        bk = 2 * pb + si
                        pe(lambda e, bk=bk, k=k, wb=wb, rap=rap, c0=c0, n=n, idx=idx: e.matmul(
                            PS[bk][0:M, 0:n], lhsT=wb[:, k, 0:M], rhs=rap[:, c0:c0 + n], start=(idx == 0), stop=(idx == total - 1)),
                           [rwb, rres], [RB[bk]])
                    idx += 1
            return pb

        def for_splits(pb, splits, fn):
            for si, (c0, n) in enumerate(splits):
                fn(PS[2 * pb + si], RB[2 * pb + si], c0, n)

        def rhs_uT(k):
            return uT[:, k, :], RuT[k]

        def rhs_h(base):
            return lambda k: (hbuf[:, base + k, :], Rh[base + k])

        def t_out(src_fn, src_res, nk, r, dst):
            for k0 in range(0, nk, 4):
                for kk in range(4):
                    pe(lambda e, k=k0 + kk, kk=kk: e.transpose(out=PS[4][0:r, kk * 128:(kk + 1) * 128], in_=src_fn(k), identity=identf[:]),
                       [src_res, Rid], [RB[4]])
                i = nxt("tst")
                act(lambda e, i=i: e.copy(out=tst[i][0:r, :], in_=PS[4][0:r, :]), [RB[4]], [Rtst[i]])
                out_events.append(load(lambda e, i=i, k0=k0: e.dma_start(out=dst[:, k0 * 128:k0 * 128 + 512], in_=tst[i][0:r, :]), Rtst[i], r=[Rtst[i]]))

        def t_in(src, r, nk, dst_fn, dst_res):
            for k0 in range(0, nk, 4):
                i = nxt("tst")
                load(lambda e, i=i, k0=k0: e.dma_start(out=tst[i][0:r, :], in_=src[:, k0 * 128:k0 * 128 + 512]), Rtst[i], w=[Rtst[i]])
                for kk in range(4):
                    pe(lambda e, i=i, kk=kk: e.transpose(out=PS[4][:, kk * 16:kk * 16 + r], in_=tst[i][0:r, kk * 128:(kk + 1) * 128], identity=identf[0:r, 0:r]),
                       [Rtst[i], Rid], [RB[4]])
                for kk in range(4):
                    act(lambda e, k=k0 + kk, kk=kk: e.copy(out=dst_fn(k), in_=PS[4][:, kk * 16:kk * 16 + r]), [RB[4]], [dst_res])

        def rms_rstd(src_fn, nchunks, T, splits, nfeat):
            pb = ring["pb"]; ring["pb"] = (pb + 1) % ring["npairs"]
            for k in range(nchunks):
                ap, rs = src_fn(k)
                i = nxt("sq")
                if k % 2 == 0:
                    act(lambda e, ap=ap, i=i: e.activation(out=sq[i][:, 0:T], in_=ap, func=AF.Square), rs if isinstance(rs, list) else [rs], [Rsq[i]])
                else:
                    dve(lambda e, ap=ap, i=i: e.tensor_tensor(out=sq[i][:, 0:T], in0=ap, in1=ap, op=ALU.mult), rs if isinstance(rs, list) else [rs], [Rsq[i]])
                for si, (c0, n) in enumerate(splits):
                    pe(lambda e, i=i, si=si, c0=c0, n=n, k=k: e.matmul(PS[2 * pb + si][:, 0:n], lhsT=ones_bf[:], rhs=sq[i][:, c0:c0 + n],
                                                                      start=(k == 0), stop=(k == nchunks - 1)), [Rsq[i], Rones], [RB[2 * pb + si]])

            def ev(ps, rb, c0, n):
                dve(lambda e: e.tensor_scalar(out=rstd[:, c0:c0 + n], in0=ps[:, 0:n], scalar1=1.0 / nfeat, scalar2=EPS, op0=ALU.mult, op1=ALU.add), [rb], [Rrstd])
            for_splits(pb, splits, ev)
            act(lambda e: e.activation(out=rstd[:, 0:T], in_=rstd[:, 0:T], func=AF.Ln), [Rrstd], [Rrstd])
            act(lambda e: e.activation(out=rstd[:, 0:T], in_=rstd[:, 0:T], func=AF.Exp, scale=-0.5), [Rrstd], [Rrstd])

        def norm_to_uT(l, gi, T, splits):
            rms_rstd(lambda k: (big1[:, k, 0:T], Rb1[k]), 16, T, splits, float(D))
            for k in range(16):
                dve(lambda e, k=k: e.scalar_tensor_tensor(out=uT[:, k, 0:T], in0=big1[:, k, 0:T], scalar=gv[:, l, gi, k:k + 1], in1=rstd[:, 0:T],
                                                          op0=ALU.mult, op1=ALU.mult), [Rb1[k], Rrstd, Rc], [RuT[k]])

        def store_x(T):
            for k in range(16):
                load(lambda e, k=k: e.dma_start(out=xs[k, :, 0:T], in_=big1[:, k, 0:T]), Rb1[k], r=[Rb1[k]], w=[Rxs])

        def prefetch_x(T):
            load(lambda e: e.dma_start(out=xpre[:, :, 0:T], in_=xs[:, :, 0:T].rearrange("k p t -> p k t")), Rxpre, r=[Rxs],
                 w=[Rxpre, RtA, RtB] + Rpraw + RdT)

        def post_norm_residual(l, gi, T, splits):
            rms_rstd(lambda k: (big1[:, k, 0:T], Rb1[k]), 16, T, splits, float(D))
            for k in range(16):
                j = nxt("stg")
                dve(lambda e, k=k, j=j: e.scalar_tensor_tensor(out=stg[j][:, 0:T], in0=big1[:, k, 0:T], scalar=gv[:, l, gi, k:k + 1], in1=rstd[:, 0:T],
                                                               op0=ALU.mult, op1=ALU.mult), [Rb1[k], Rrstd, Rc], [Rstg[j]])
                em.op("pool" if k % 3 == 0 else "dve", lambda e, k=k, j=j: e.tensor_tensor(out=big1[:, k, 0:T], in0=stg[j][:, 0:T], in1=xpre[:, k, 0:T], op=ALU.add),
                      reads=[Rstg[j], Rxpre], writes=[Rb1[k]])

        for b in range(NBLK):
            lastp = (b == NBLK - 1)
            has_s = (b == 0)
            T = 544 if has_s else 512
            splits = [(0, 512), (512, 32)] if has_s else [(0, 512)]
            segs = [("p", 0, 512)] + ([("a", 512, 16), ("b", 528, 16)] if has_s else [])
            nseg = len(segs)
            chunks = [(0, 64 * c, 64) for c in range(8)] + ([(1, 512, 16), (2, 528, 16)] if has_s else [])

            ttiles = [(xp, 512 * b + 128 * t, 128, 128 * t) for t in range(4)] + ([(xsm, 0, 32, 512)] if has_s else [])
            for (srcT, r0, nr, c0) in ttiles:
                for q in range(4):
                    i = nxt("tst")
                    load(lambda e, i=i, q=q, srcT=srcT, r0=r0, nr=nr: e.dma_start(out=tst[i][0:nr, :], in_=srcT[r0:r0 + nr, 512 * q:512 * q + 512]), Rtst[i], w=[Rtst[i]])
                    for kk in range(4):
                        pe(lambda e, i=i, kk=kk, nr=nr: e.transpose(out=PS[4][:, kk * 128:kk * 128 + nr], in_=tst[i][0:nr, kk * 128:(kk + 1) * 128], identity=identf[0:nr, 0:nr]),
                           [Rtst[i], Rid], [RB[4]])
                    for kk in range(4):
                        act(lambda e, k=4 * q + kk, kk=kk, nr=nr, c0=c0: e.copy(out=big1[:, k, c0:c0 + nr], in_=PS[4][:, kk * 128:kk * 128 + nr]), [RB[4]], [Rb1[4 * q + kk]])

            for l in range(2):
                store_x(T)
                norm_to_uT(l, 0, T, splits)
                em.barrier()
                ring["npairs"] = 2; ring["pb"] = 0
                if has_s:
                    for sb in range(2):
                        t_in(stc[l, sb], 3, 48, lambda k, sb=sb: sp_conv[sb][:, k, :], Rsp_conv[sb])
                wsrc, wkey = wslab(w_in, l, 0, 16, OFF_DT, 64)
                wb_, rwb_ = load_slab(wsrc, 16, 64, wkey)
                act(lambda e, wb_=wb_: e.copy(out=wdt[:], in_=wb_[:, :, 0:64]), [rwb_], [Rwdt])
                grp = [(5, 64, [(ci, c0) for ci, (sid, c0, Q) in enumerate(chunks) if Q == 64])]
                if has_s:
                    grp.append((6, 16, [(ci, c0) for ci, (sid, c0, Q) in enumerate(chunks) if Q == 16]))
                for (bk, Q, lst) in grp:
                    nq = len(lst); ci0 = lst[0][0]
                    for qi, (ci, c0) in enumerate(lst):
                        for k in range(16):
                            pe(lambda e, k=k, c0=c0, Q=Q, qi=qi, bk=bk: e.matmul(PS[bk][0:Q, qi * 64:(qi + 1) * 64], lhsT=uT[:, k, c0:c0 + Q], rhs=wdt[:, k, :],
                                                                                 start=(k == 0), stop=(k == 15)), [RuT[k], Rwdt], [RB[bk]])
                    dve(lambda e, Q=Q, nq=nq, bk=bk: e.tensor_tensor(out=dtmp[0:Q, 0:nq * 64].rearrange("q (c h) -> q c h", c=nq),
                                                                     in0=PS[bk][0:Q, 0:nq * 64].rearrange("q (c h) -> q c h", c=nq),
                                                                     in1=dtb[0:Q, l, :][:, None, :].broadcast_to([Q, nq, 64]), op=ALU.add), [RB[bk], Rc], [Rdtmp])
                    act(lambda e, Q=Q, nq=nq: e.activation(out=dtmp[0:Q, 0:nq * 64], in_=dtmp[0:Q, 0:nq * 64], func=AF.Exp), [Rdtmp], [Rdtmp])
                    act(lambda e, Q=Q, nq=nq, ci0=ci0: e.activation(out=dt_tm[0:Q, ci0:ci0 + nq, :], in_=dtmp[0:Q, 0:nq * 64].rearrange("q (c h) -> q c h", c=nq),
                                                                    func=AF.Ln, bias=1.0), [Rdtmp], [Rdt])
                    dve(lambda e, Q=Q, nq=nq, ci0=ci0: e.tensor_tensor(out=a_tm[0:Q, ci0:ci0 + nq, :], in0=dt_tm[0:Q, ci0:ci0 + nq, :],
                                                                       in1=Abc[0:Q, l, :][:, None, :].broadcast_to([Q, nq, 64]), op=ALU.mult), [Rdt, Rc], [Rdt])
                    dve(lambda e, Q=Q, nq=nq, ci0=ci0: e.tensor_copy(out=a_hl[0:Q, ci0:ci0 + nq, 0, :], in_=a_tm[0:Q, ci0:ci0 + nq, :]), [Rdt], [Rdt])
                    dve(lambda e, Q=Q, nq=nq, ci0=ci0: e.tensor_tensor(out=a_hl[0:Q, ci0:ci0 + nq, 1, :], in0=a_tm[0:Q, ci0:ci0 + nq, :],
                                                                       in1=a_hl[0:Q, ci0:ci0 + nq, 0, :], op=ALU.subtract), [Rdt], [Rdt])

                def conv_chunk(l, kc, dst_fn, dst_res):
                    pb = mm_group([(wslab(w_in, l, 0, 16, OFF_X + 128 * kc, 128), 16, rhs_uT)], 128, splits)
                    i = nxt("raw")
                    for si, (kind, c0, n) in enumerate(segs):
                        p0 = c0 + 3 * si
                        if kind == "p":
                            if b == 0:
                                pool(lambda e, i=i, p0=p0: e.memset(raw[i][:, p0:p0 + 3], 0.0), [], [Rraw[i]])
                            else:
                                pool(lambda e, i=i, p0=p0: e.tensor_copy(out=raw[i][:, p0:p0 + 3], in_=ccar[l][:, kc, :]), [Rccar[l]], [Rraw[i]])
                        else:
                            sb = 0 if kind == "a" else 1
                            pool(lambda e, i=i, p0=p0, sb=sb: e.tensor_copy(out=raw[i][:, p0:p0 + 3], in_=sp_conv[sb][:, kc, :]), [Rsp_conv[sb]], [Rraw[i]])
                    for si, (kind, c0, n) in enumerate(segs):
                        p0 = c0 + 3 * si + 3
                        bk = 2 * pb + (0 if c0 < 512 else 1)
                        cc = c0 if c0 < 512 else c0 - 512
                        act(lambda e, i=i, p0=p0, n=n, bk=bk, cc=cc: e.copy(out=raw[i][:, p0:p0 + n], in_=PS[bk][:, cc:cc + n]), [RB[bk]], [Rraw[i]])
                    for si, (kind, c0, n) in enumerate(segs):
                        pe_ = c0 + 3 * si + 3 + n
                        if kind == "p":
                            pool(lambda e, i=i, pe_=pe_: e.tensor_copy(out=ccar[l][:, kc, :], in_=raw[i][:, pe_ - 3:pe_]), [Rraw[i]], [Rccar[l]])
                        else:
                            sb = 0 if kind == "a" else 1
                            pool(lambda e, i=i, pe_=pe_, sb=sb: e.tensor_copy(out=sco[sb][:, kc, :], in_=raw[i][:, pe_ - 3:pe_]), [Rraw[i]], [Rsco[sb]])
                    return lambda: conv_tail(l, kc, i, dst_fn, dst_res)

                def conv_tail(l, kc, i, dst_fn, dst_res):
                    L = T + 3 * nseg - 3
                    veng = "dve"
                    em.op(veng, lambda e, i=i: e.tensor_scalar(out=cacc[i][:, 0:L], in0=raw[i][:, 0:L], scalar1=cw[:, l, 0, kc:kc + 1], scalar2=None, op0=ALU.mult),
                          reads=[Rraw[i], Rc], writes=[Rcacc[i]])
                    for j in range(1, 4):
                        em.op(veng, lambda e, i=i, j=j: e.scalar_tensor_tensor(out=cacc[i][:, 0:L], in0=raw[i][:, j:j + L], scalar=cw[:, l, j, kc:kc + 1],
                                                                              in1=cacc[i][:, 0:L], op0=ALU.mult, op1=ALU.add),
                              reads=[Rraw[i], Rcacc[i], Rc], writes=[Rcacc[i]])
                    for si, (kind, c0, n) in enumerate(segs):
                        a0 = c0 + 3 * si
                        act(lambda e, i=i, a0=a0, n=n, c0=c0: e.activation(out=dst_fn(c0, n), in_=cacc[i][:, a0:a0 + n], func=AF.Silu, bias=cbias[:, l, kc:kc + 1]),
                            [Rcacc[i], Rc], dst_res if isinstance(dst_res, list) else [dst_res])

                for g in range(NG):
                    jobs = [(4 * g + j, (lambda c0, n, j=j: xg[:, j, c0:c0 + n]), Rxgc[0:len(chunks)]) for j in range(4)]
                    jobs += [(32 + g, (lambda c0, n: BT[:, c0:c0 + n]), RBT), (40 + g, (lambda c0, n: CT[:, c0:c0 + n]), RCT)]
                    pend = None
                    for (kc_, dfn, dres) in jobs:
                        tail = conv_chunk(l, kc_, dfn, dres)
                        if pend is not None:
                            pend()
                        pend = tail
                    pend()
                    act(lambda e: e.copy(out=BTb[:, 0:T], in_=BT[:, 0:T]), [RBT], [RBTb])
                    act(lambda e: e.copy(out=CTb[:, 0:T], in_=CT[:, 0:T]), [RCT], [RCTb])

                    for j in range(4):
                        pool(lambda e, j=j: e.tensor_scalar(out=Dg[:, j, :], in0=identf[:], scalar1=Dp[:, l, 4 * g + j:4 * g + j + 1], scalar2=None, op0=ALU.mult),
                             [Rid, Rc], [RDg])
                    NC_ = len(chunks)
                    st_ = {"cur": -1, "fin": None}
                    BU = 1
                    BC = 3

                    def prep(ci, g=g):
                        sid, c0, Q = chunks[ci]
                        cc = ci % NCH
                        RC = Rc_[cc]
                        xdt, xw, btm, cbm, Rm, Mexp, Mb, Et, Cs = c_xdt[cc], c_xw[cc], c_btm[cc], c_cbm[cc], c_R[cc], c_Mexp[cc], c_Mb[cc], c_E[cc], c_Cs[cc]
                        Q8 = 8 * Q
                        for j in range(4):
                            pe(lambda e, j=j: e.transpose(out=PS[4][0:Q, j * 128:(j + 1) * 128], in_=xg[:, j, c0:c0 + Q], identity=identf[:]), [Rxgc[ci], Rid], [RB[4]])
                        dve(lambda e: e.tensor_tensor(out=xdt[0:Q, :].rearrange("q (h p) -> q h p", h=8), in0=PS[4][0:Q, :].rearrange("q (h p) -> q h p", h=8),
                                                      in1=dt_tm[0:Q, ci, 8 * g:8 * g + 8][:, :, None].broadcast_to([Q, 8, 64]), op=ALU.mult),
                            [RB[4], Rdt], [RC["xdt"]])
                        pe(lambda e: e.transpose(out=PS[BC][0:Q, 320:448], in_=BT[:, c0:c0 + Q], identity=identf[:]), [RBT, Rid], [RB[BC]])
                        act(lambda e: e.copy(out=btm[0:Q, :], in_=PS[BC][0:Q, 320:448]), [RB[BC]], [RC["btm"]])
                        pe(lambda e: e.matmul(PS[BC][0:Q, 256:256 + Q], lhsT=BTb[:, c0:c0 + Q], rhs=CTb[:, c0:c0 + Q], start=True, stop=True), [RBTb, RCTb], [RB[BC]])
                        dve(lambda e: e.tensor_tensor(out=cbm[0:Q, 0:Q], in0=PS[BC][0:Q, 256:256 + Q], in1=mask01[0:Q, 0:Q], op=ALU.mult), [RB[BC], Rc], [RC["cbm"]])
                        pool(lambda e: e.tensor_tensor(out=Rm[0:Q, 0:2 * Q8].rearrange("q (w h t) -> q w h t", w=2, h=8),
                                                       in0=a_hl[0:Q, ci, :, 8 * g:8 * g + 8][:, :, :, None].broadcast_to([Q, 2, 8, Q]),
                                                       in1=mask01[0:Q, 0:Q][:, None, None, :].broadcast_to([Q, 2, 8, Q]), op=ALU.mult),
                             [Rdt, Rc], [RC["R"]])
                        for w_ in range(2):
                            pe(lambda e, w_=w_: e.matmul(PS[5][0:Q, 0:Q8], lhsT=ugt_bf[0:Q, 0:Q], rhs=Rm[0:Q, w_ * Q8:(w_ + 1) * Q8], start=(w_ == 0), stop=(w_ == 1)),
                               [RC["R"], Rc], [RB[5]])
                        for w_ in range(2):
                            pe(lambda e, w_=w_: e.matmul(PS[6][:, 0:Q8], lhsT=ones_bf[0:Q, :], rhs=Rm[0:Q, w_ * Q8:(w_ + 1) * Q8], start=(w_ == 0), stop=(w_ == 1)),
                               [RC["R"], Rones], [RB[6]])
                        act(lambda e: e.activation(out=Mexp[0:Q, 0:Q8], in_=PS[5][0:Q, 0:Q8], func=AF.Exp), [RB[5]], [RC["Mexp"]])
                        act(lambda e: e.activation(out=Et[:, 0:Q8], in_=PS[6][:, 0:Q8], func=AF.Exp), [RB[6]], [RC["E"]])

                    def prep_b(ci, g=g):
                        sid, c0, Q = chunks[ci]
                        cc = ci % NCH
                        RC = Rc_[cc]
                        xdt, xw, btm, cbm, Rm, Mexp, Mb, Et, Cs = c_xdt[cc], c_xw[cc], c_btm[cc], c_cbm[cc], c_R[cc], c_Mexp[cc], c_Mb[cc], c_E[cc], c_Cs[cc]
                        Q8 = 8 * Q
                        dve(lambda e: e.tensor_tensor(out=Mb[0:Q, 0:Q8].rearrange("q (h t) -> q h t", h=8), in0=Mexp[0:Q, 0:Q8].rearrange("q (h t) -> q h t", h=8),
                                                      in1=cbm[0:Q, 0:Q][:, None, :].broadcast_to([Q, 8, Q]), op=ALU.mult),
                            [RC["Mexp"], RC["cbm"]], [RC["Mb"]])
                        pool(lambda e: e.tensor_tensor(out=Cs[:, 0:Q8].rearrange("n (h t) -> n h t", h=8), in0=Et[:, 0:Q8].rearrange("n (h t) -> n h t", h=8),
                                                       in1=CT[:, c0:c0 + Q][:, None, :].broadcast_to([128, 8, Q]), op=ALU.mult),
                             [RC["E"], RCT], [RC["Cs"]])
                        pool(lambda e: e.tensor_tensor(out=xw[0:Q, :].rearrange("q (h p) -> q h p", h=8), in0=xdt[0:Q, :].rearrange("q (h p) -> q h p", h=8),
                                                       in1=Mexp[0:Q, 0:Q8].rearrange("q (h t) -> q h t", h=8)[:, :, Q - 1:Q].broadcast_to([Q, 8, 64]), op=ALU.mult),
                             [RC["xdt"], RC["Mexp"]], [RC["xw"]])

                    def finish_state(sid_, g=g):
                        if sid_ == 0 and not lastp:
                            load(lambda e: e.dma_start(out=hs[l, g], in_=hT[:]), RhT, r=[RhT], w=[Rhs[l][g]])
                            return
                        dstT = nsp[l] if sid_ == 0 else nss[l, sid_ - 1]
                        for jj in range(4):
                            pe(lambda e, jj=jj: e.transpose(out=PS[4][:, jj * 128:(jj + 1) * 128], in_=hT[:, jj * 128:(jj + 1) * 128], identity=identf[:]), [RhT, Rid], [RB[4]])
                        act(lambda e: e.copy(out=hstg[:].rearrange("p j n -> p (j n)"), in_=PS[4][:, :]), [RB[4]], [Rhstg])
                        out_events.append(load(lambda e: e.dma_start(out=dstT[512 * g:512 * g + 512, :].rearrange("(j p) n -> p j n", p=128), in_=hstg[:]),
                                               Rhstg, r=[Rhstg]))

                    def scan(ci, g=g):
                        sid, c0, Q = chunks[ci]
                        if sid != st_["cur"]:
                            if st_["cur"] >= 0:
                                finish_state(st_["cur"])
                            st_["cur"] = sid
                            if sid == 0:
                                if b == 0:
                                    dve(lambda e: e.memset(hT[:], 0.0), [], [RhT])
                                else:
                                    load(lambda e: e.dma_start(out=hT[:], in_=hs[l, g]), RhT, r=[Rhs[l][g]], w=[RhT])
                            else:
                                sb = sid - 1
                                load(lambda e: e.dma_start(out=hstg[:], in_=sts[l, sb, 512 * g:512 * g + 512, :].rearrange("(j p) n -> p j n", p=128)),
                                     Rhstg, w=[Rhstg])
                                for jj in range(4):
                                    pe(lambda e, jj=jj: e.transpose(out=PS[4][:, jj * 128:(jj + 1) * 128], in_=hstg[:, jj, :], identity=identf[:]), [Rhstg, Rid], [RB[4]])
                                dve(lambda e: e.tensor_copy(out=hT[:], in_=PS[4][:, :]), [RB[4]], [RhT])
                            dve(lambda e: e.tensor_copy(out=hTb2[ci % 2][:], in_=hT[:]), [RhT], [RhTb2[ci % 2]])
                        cc = ci % NCH
                        RC = Rc_[cc]
                        xdt, xw, btm, Mb, Et, Cs = c_xdt[cc], c_xw[cc], c_btm[cc], c_Mb[cc], c_E[cc], c_Cs[cc]
                        Q8 = 8 * Q
                        hb_old, rhb_old = hTb2[ci % 2], RhTb2[ci % 2]
                        hb_new, rhb_new = hTb2[(ci + 1) % 2], RhTb2[(ci + 1) % 2]
                        pe(lambda e: e.matmul(PS[BU][:, :], lhsT=btm[0:Q, :], rhs=xw[0:Q, :], start=True, stop=True), [RC["btm"], RC["xw"]], [RB[BU]])
                        for j in range(4):
                            pe(lambda e, j=j: e.matmul(PS[7][:, j * Q:(j + 1) * Q], lhsT=Dg[:, j, :], rhs=xg[:, j, c0:c0 + Q], start=(j == 0), stop=False, skip_group_check=True),
                               [RDg, Rxgc[ci]], [RB[7]])
                        for hh in range(8):
                            po = (hh % 2) * 64; jq = hh // 2
                            pe(lambda e, hh=hh, po=po, jq=jq: e.matmul(PS[7][po:po + 64, jq * Q:(jq + 1) * Q], lhsT=xdt[0:Q, hh * 64:(hh + 1) * 64],
                                                                      rhs=Mb[0:Q, hh * Q:(hh + 1) * Q], start=False, stop=False, skip_group_check=True),
                               [RC["xdt"], RC["Mb"]], [RB[7]])
                        for hh in range(8):
                            po = (hh % 2) * 64; jq = hh // 2
                            pe(lambda e, hh=hh, po=po, jq=jq: e.matmul(PS[7][po:po + 64, jq * Q:(jq + 1) * Q], lhsT=hb_old[:, hh * 64:(hh + 1) * 64],
                                                                      rhs=Cs[:, hh * Q:(hh + 1) * Q], start=False, stop=True, skip_group_check=True),
                               [rhb_old, RC["Cs"]], [RB[7]])
                        dve(lambda e: e.tensor_tensor(out=c_htmp[:].rearrange("n (h p) -> n h p", h=8), in0=hT[:].rearrange("n (h p) -> n h p", h=8),
                                                      in1=Et[:, 0:Q8].rearrange("n (h t) -> n h t", h=8)[:, :, Q - 1:Q].broadcast_to([128, 8, 64]), op=ALU.mult),
                            [RhT, RC["E"]], [Rhtmp])
                        dve(lambda e: e.tensor_tensor(out=hT[:], in0=c_htmp[:], in1=PS[BU][:, :], op=ALU.add), [Rhtmp, RB[BU]], [RhT])
                        dve(lambda e: e.tensor_copy(out=hb_new[:], in_=hT[:]), [RhT], [rhb_new])
                        dve(lambda e: e.tensor_tensor(out=zs4[:, :, c0:c0 + Q], in0=PS[7][:, 0:4 * Q].rearrange("p (j t) -> p j t", j=4), in1=zs4[:, :, c0:c0 + Q], op=ALU.mult),
                            [RB[7]] + Rzs, Rzs)

                    def z_chunk(j, g=g):
                        pb = mm_group([(wslab(w_in, l, 0, 16, OFF_Z + 512 * g + 128 * j, 128), 16, rhs_uT)], 128, splits)

                        def ev(ps, rb, c0, n):
                            act(lambda e: e.activation(out=zs[j][:, c0:c0 + n], in_=ps[:, 0:n], func=AF.Silu), [rb], [Rzs[j]])
                        for_splits(pb, splits, ev)

                    for j in range(4):
                        z_chunk(j)
                    prep(0); prep(1); prep_b(0)
                    for ci in range(NC_):
                        if ci + 2 < NC_:
                            prep(ci + 2)
                        scan(ci)
                        if ci + 1 < NC_:
                            prep_b(ci + 1)
                    finish_state(st_["cur"])

                    rms_rstd(lambda k: (zs[k][:, 0:T], Rzs[k]), 4, T, splits, 512.0)
                    for j in range(4):
                        dve(lambda e, j=j: e.scalar_tensor_tensor(out=hbuf[:, 4 * g + j, 0:T], in0=zs[j][:, 0:T], scalar=nw[:, l, 4 * g + j:4 * g + j + 1], in1=rstd[:, 0:T],
                                                                  op0=ALU.mult, op1=ALU.mult), [Rzs[j], Rrstd, Rc], [Rh[4 * g + j]])

                if lastp:
                    t_out(lambda k: ccar[l][:, k, :], Rccar[l], 48, 3, ncp[l])
                if has_s:
                    for sb in range(2):
                        t_out(lambda k, sb=sb: sco[sb][:, k, :], Rsco[sb], 48, 3, ncs[l, sb])
                em.barrier()
                ring["npairs"] = 4

                for n in range(16):
                    pbx = mm_group([(wslab(w_sp, l, 2048 * q, 16, 128 * n, 128), 16, rhs_h(16 * q)) for q in range(2)], 128, splits)
                    pby = mm_group([(wslab(w_in, l, 0, 16, OFF_GB + 128 * n, 128), 16, rhs_uT)], 128, splits)
                    i = nxt("stg")
                    for si, (c0, nn) in enumerate(splits):
                        act(lambda e, i=i, si=si, c0=c0, nn=nn: e.activation(out=stg[i][:, c0:c0 + nn], in_=PS[2 * pby + si][:, 0:nn], func=AF.Sigmoid), [RB[2 * pby + si]], [Rstg[i]])
                        dve(lambda e, i=i, si=si, c0=c0, nn=nn, n=n: e.tensor_tensor(out=big1[:, n, c0:c0 + nn], in0=stg[i][:, c0:c0 + nn], in1=PS[2 * pbx + si][:, 0:nn], op=ALU.mult),
                            [Rstg[i], RB[2 * pbx + si]], [Rb1[n]])

                if has_s:
                    for sb in range(2):
                        t_in(stp[l, sb], 15, 16, lambda k, sb=sb: sp_pool[sb][:, k, :], Rsp_pool[sb])
                for gi in range(4):
                    wnd = 2 << gi
                    di = nxt("dT")
                    for j in range(4):
                        kc = 4 * gi + j
                        pb = mm_group([(wslab(w_in, l, 0, 16, 128 * kc, 128), 16, rhs_uT)], 128, splits)
                        i = nxt("praw")
                        for si, (kind, c0, n) in enumerate(segs):
                            p0 = c0 + 15 * si
                            if kind == "p":
                                if b == 0:
                                    pool(lambda e, i=i, p0=p0: e.memset(praw[i][:, p0:p0 + 15], 0.0), [], [Rpraw[i]])
                                else:
                                    pool(lambda e, i=i, p0=p0, kc=kc: e.tensor_copy(out=praw[i][:, p0:p0 + 15], in_=pcar[l][:, kc, :]), [Rpcar[l]], [Rpraw[i]])
                            else:
                                sb = 0 if kind == "a" else 1
                                pool(lambda e, i=i, p0=p0, sb=sb, kc=kc: e.tensor_copy(out=praw[i][:, p0:p0 + 15], in_=sp_pool[sb][:, kc, :]), [Rsp_pool[sb]], [Rpraw[i]])
                        for si, (kind, c0, n) in enumerate(segs):
                            p0 = c0 + 15 * si + 15
                            bk = 2 * pb + (0 if c0 < 512 else 1)
                            cc_ = c0 if c0 < 512 else c0 - 512
                            act(lambda e, i=i, p0=p0, n=n, bk=bk, cc_=cc_: e.copy(out=praw[i][:, p0:p0 + n], in_=PS[bk][:, cc_:cc_ + n]), [RB[bk]], [Rpraw[i]])
                        for si, (kind, c0, n) in enumerate(segs):
                            pe_ = c0 + 15 * si + 15 + n
                            if kind == "p":
                                pool(lambda e, i=i, pe_=pe_, kc=kc: e.tensor_copy(out=pcar[l][:, kc, :], in_=praw[i][:, pe_ - 15:pe_]), [Rpraw[i]], [Rpcar[l]])
                            else:
                                sb = 0 if kind == "a" else 1
                                pool(lambda e, i=i, pe_=pe_, sb=sb, kc=kc: e.tensor_copy(out=spo[sb][:, kc, :], in_=praw[i][:, pe_ - 15:pe_]), [Rpraw[i]], [Rspo[sb]])
                        P = T + 15 * nseg
                        src, rsrc = praw[i], Rpraw[i]
                        bufs = [(tA, RtA), (tB, RtB)]
                        sh = 1; bi = 0; valid = 0
                        while sh < wnd:
                            dstb, rdst = bufs[bi]; bi = 1 - bi
                            v0 = valid + sh
                            dve(lambda e, src=src, dstb=dstb, v0=v0, sh=sh, P=P: e.tensor_tensor(out=dstb[:, v0:P], in0=src[:, v0:P], in1=src[:, v0 - sh:P - sh], op=ALU.add),
                                [rsrc], [rdst])
                            src, rsrc = dstb, rdst
                            valid = v0; sh *= 2
                        for si, (kind, c0, n) in enumerate(segs):
                            p0 = c0 + 15 * si + 15
                            dve(lambda e, src=src, p0=p0, n=n, c0=c0, i=i, j=j, di=di, wnd=wnd: e.scalar_tensor_tensor(
                                out=dTt[di][:, j, c0:c0 + n], in0=src[:, p0:p0 + n], scalar=1.0 / wnd, in1=praw[i][:, p0:p0 + n], op0=ALU.mult, op1=ALU.subtract),
                                [rsrc, Rpraw[i]], [RdT[di]])
                        if b == 0 and wnd > 2:
                            m = wnd - 1
                            js = nxt("stg")
                            dve(lambda e, src=src, m=m, js=js: e.tensor_tensor(out=stg[js][:, 0:m], in0=src[:, 15:15 + m], in1=invc[:, 0:m], op=ALU.mult), [rsrc, Rc], [Rstg[js]])
                            dve(lambda e, m=m, js=js, i=i, j=j, di=di: e.tensor_tensor(out=dTt[di][:, j, 0:m], in0=stg[js][:, 0:m], in1=praw[i][:, 15:15 + m], op=ALU.subtract),
                                [Rstg[js], Rpraw[i]], [RdT[di]])
                        elif b == 0:
                            js = nxt("stg")
                            dve(lambda e, src=src, js=js: e.tensor_tensor(out=stg[js][:, 0:1], in0=src[:, 15:16], in1=invc[:, 0:1], op=ALU.mult), [rsrc, Rc], [Rstg[js]])
                            dve(lambda e, js=js, i=i, j=j, di=di: e.tensor_tensor(out=dTt[di][:, j, 0:1], in0=stg[js][:, 0:1], in1=praw[i][:, 15:16], op=ALU.subtract),
                                [Rstg[js], Rpraw[i]], [RdT[di]])
                    for jo in range(4):
                        srcw = (pool_w[l, gi, :, 128 * jo:128 * jo + 128].rearrange("(k p) n -> p k n", p=128), ("pool_w", l, gi, jo))
                        pb = mm_group([(srcw, 4, lambda k, di=di: (dTt[di][:, k, :], RdT[di]))], 128, splits)

                        def ev(ps, rb, c0, n, jo=jo):
                            act(lambda e: e.activation(out=hbuf[:, 4 * gi + jo, c0:c0 + n], in_=ps[:, 0:n], func=AF.Copy, scale=psc[:, l, 4 * gi + jo:4 * gi + jo + 1]),
                                [rb, Rc], [Rh[4 * gi + jo]])
                        for_splits(pb, splits, ev)
                if lastp:
                    t_out(lambda k: pcar[l][:, k, :], Rpcar[l], 16, 15, npp[l])
                if has_s:
                    for sb in range(2):
                        t_out(lambda k, sb=sb: spo[sb][:, k, :], Rspo[sb], 16, 15, nps[l, sb])

                for n in range(16):
                    pbx = mm_group([(wslab(w_pp, l, 0, 16, 128 * n, 128), 16, rhs_h(0))], 128, splits)
                    pby = mm_group([(wslab(w_in, l, 0, 16, OFF_GA + 128 * n, 128), 16, rhs_uT)], 128, splits)
                    i = nxt("stg")
                    for si, (c0, nn) in enumerate(splits):
                        act(lambda e, i=i, si=si, c0=c0, nn=nn: e.activation(out=stg[i][:, c0:c0 + nn], in_=PS[2 * pby + si][:, 0:nn], func=AF.Sigmoid), [RB[2 * pby + si]], [Rstg[i]])
                        dve(lambda e, i=i, si=si, c0=c0, nn=nn: e.tensor_tensor(out=stg[i][:, c0:c0 + nn], in0=stg[i][:, c0:c0 + nn], in1=PS[2 * pbx + si][:, 0:nn], op=ALU.mult),
                            [Rstg[i], RB[2 * pbx + si]], [Rstg[i]])
                    pool(lambda e, i=i, n=n: e.tensor_tensor(out=hbuf[:, 16 + n, 0:T], in0=stg[i][:, 0:T], in1=big1[:, n, 0:T], op=ALU.add), [Rstg[i], Rb1[n]], [Rh[16 + n]])

                prefetch_x(T)
                for n in range(16):
                    pb = mm_group([(wslab(w_out, l, 0, 16, 128 * n, 128), 16, rhs_h(16))], 128, splits)

                    def ev(ps, rb, c0, nn, n=n):
                        act(lambda e: e.copy(out=big1[:, n, c0:c0 + nn], in_=ps[:, 0:nn]), [rb], [Rb1[n]])
                    for_splits(pb, splits, ev)
                post_norm_residual(l, 1, T, splits)
                store_x(T)

                norm_to_uT(l, 2, T, splits)
                for hh in range(2):
                    for n in range(32):
                        pb = mm_group([(wslab(w_up, l, 0, 16, 4096 * hh + 128 * n, 128), 16, rhs_uT)], 128, splits)
                        i = nxt("stg")

                        def ev(ps, rb, c0, nn, i=i):
                            act(lambda e: e.activation(out=stg[i][:, c0:c0 + nn], in_=ps[:, 0:nn], func=AF.Relu), [rb], [Rstg[i]])
                        for_splits(pb, splits, ev)
                        pool(lambda e, i=i, n=n: e.tensor_tensor(out=hbuf[:, n, 0:T], in0=stg[i][:, 0:T], in1=stg[i][:, 0:T], op=ALU.mult), [Rstg[i]], [Rh[n]])
                    if hh == 1:
                        prefetch_x(T)
                    for n in range(16):
                        pb = mm_group([(wslab(w_down, l, 4096 * hh + 2048 * q, 16, 128 * n, 128), 16, rhs_h(16 * q)) for q in range(2)], 128, splits)

                        def ev(ps, rb, c0, nn, n=n, hh=hh):
                            if hh == 0:
                                act(lambda e: e.copy(out=big1[:, n, c0:c0 + nn], in_=ps[:, 0:nn]), [rb], [Rb1[n]])
                            else:
                                dve(lambda e: e.tensor_tensor(out=big1[:, n, c0:c0 + nn], in0=big1[:, n, c0:c0 + nn], in1=ps[:, 0:nn], op=ALU.add), [rb, Rb1[n]], [Rb1[n]])
                        for_splits(pb, splits, ev)
                post_norm_residual(l, 3, T, splits)

            for (srcT, r0, nr, c0) in ttiles:
                dstT = yp if srcT is xp else ysm
                for q in range(4):
                    for kk in range(4):
                        pe(lambda e, k=4 * q + kk, kk=kk, nr=nr, c0=c0: e.transpose(out=PS[4][0:nr, kk * 128:(kk + 1) * 128], in_=big1[:, k, c0:c0 + nr], identity=identf[:]),
                           [Rb1[4 * q + kk], Rid], [RB[4]])
                    i = nxt("tst")
                    act(lambda e, i=i, nr=nr: e.copy(out=tst[i][0:nr, :], in_=PS[4][0:nr, :]), [RB[4]], [Rtst[i]])
                    out_events.append(load(lambda e, i=i, q=q, dstT=dstT, r0=r0, nr=nr: e.dma_start(out=dstT[r0:r0 + nr, 512 * q:512 * q + 512], in_=tst[i][0:nr, :]),
                                           Rtst[i], r=[Rtst[i]]))

        em.wait_all("sp", out_events)

    return nc


_NC_CACHE = {}


def kernel(x_prompt, x_sample, state_pool, state_conv, state_ssm, w_in, pool_w, pool_scale, conv_w, conv_b,
           dt_bias, a_log, d_skip, ssd_norm, w_pool_proj, w_ssd_proj, w_out, g_mix_pre, g_mix_post,
           w_up, w_down, g_mlp_pre, g_mlp_post):
    f = lambda a: np.ascontiguousarray(np.asarray(a, dtype=np.float32))
    x_prompt = f(x_prompt); x_sample = f(x_sample); state_pool = f(state_pool); state_conv = f(state_conv); state_ssm = f(state_ssm)
    shared = {"w_in": f(w_in), "pool_w": f(pool_w), "pool_scale": f(pool_scale), "conv_w": f(conv_w), "conv_b": f(conv_b),
              "dt_bias": f(dt_bias), "a_log": f(a_log), "d_skip": f(d_skip), "ssd_norm": f(ssd_norm),
              "w_pool_proj": f(w_pool_proj), "w_ssd_proj": f(w_ssd_proj), "w_out": f(w_out),
              "g_mix_pre": f(g_mix_pre), "g_mix_post": f(g_mix_post), "g_mlp_pre": f(g_mlp_pre), "g_mlp_post": f(g_mlp_post),
              "w_up": f(w_up), "w_down": f(w_down)}
    if "nc" not in _NC_CACHE:
        _NC_CACHE["nc"] = build_program()
    nc = _NC_CACHE["nc"]
    in_maps = []
    for i in range(8):
        m = dict(shared)
        m["xp"] = x_prompt[i]
        m["xsm"] = np.ascontiguousarray(x_sample[2 * i:2 * i + 2].reshape(32, D))
        m["stp"] = np.ascontiguousarray(state_pool[:, 2 * i:2 * i + 2])
        m["stc"] = np.ascontiguousarray(state_conv[:, 2 * i:2 * i + 2])
        m["sts"] = np.ascontiguousarray(state_ssm[:, 2 * i:2 * i + 2].reshape(2, 2, DI, NS))
        in_maps.append(m)
    res = run_bass_kernel_spmd(nc, in_maps, core_ids=list(range(8)))
    R = res.results
    y_p = np.stack([R[i]["yp"] for i in range(8)], 0)
    y_s = np.concatenate([R[i]["ysm"].reshape(2, 16, D) for i in range(8)], 0)
    npp = np.stack([R[i]["npp"] for i in range(8)], 1)
    ncp = np.stack([R[i]["ncp"] for i in range(8)], 1)
    nsp = np.stack([R[i]["nsp"].reshape(2, NH, HD, NS) for i in range(8)], 1)
    nps = np.concatenate([R[i]["nps"] for i in range(8)], 1)
    ncs = np.concatenate([R[i]["ncs"] for i in range(8)], 1)
    nss = np.concatenate([R[i]["nss"].reshape(2, 2, NH, HD, NS) for i in range(8)], 1)
    return (y_p.astype(np.float32), y_s.astype(np.float32), npp.astype(np.float32), ncp.astype(np.float32),
            nsp.astype(np.float32), nps.astype(np.float32), ncs.astype(np.float32), nss.astype(np.float32))
```
